# Optimizing a Trainium2 kernel written in Bass

```python
import math
import jax, jax.numpy as jnp
from jax import lax
import numpy as np

D_MODEL = 2048
BATCH = 2
SEQ = 4096
DEPTH = 4

PLE_DIM = 256
ATTN_WIDTH = D_MODEL // 2
POOL_WIDTH = D_MODEL - ATTN_WIDTH
HEAD_DIM = 64
N_HEADS = ATTN_WIDTH // HEAD_DIM
N_KV_HEADS = max(1, N_HEADS // 8)
KV_GROUP = N_HEADS // N_KV_HEADS
WINDOW = 128
BLOCK = WINDOW
POOL_WINDOWS = (2, 4, 8, 16)
N_POOL_GROUPS = len(POOL_WINDOWS)
POOL_GROUP_DIM = POOL_WIDTH // N_POOL_GROUPS
REL_BUCKETS = 32
REL_MAX_DIST = 128
LN_EPS = 1e-5
DEEPNORM_ALPHA = (2.0 * DEPTH) ** 0.25
DEEPNORM_BETA = (8.0 * DEPTH) ** -0.25
Q_COLS = N_HEADS * HEAD_DIM
KV_COLS = N_KV_HEADS * HEAD_DIM
SPLIT_SIZES = (Q_COLS, KV_COLS, KV_COLS, ATTN_WIDTH, POOL_WIDTH, POOL_WIDTH)
SPLIT_POINTS = tuple(int(c) for c in np.cumsum(SPLIT_SIZES)[:-1])
IN_COLS = int(sum(SPLIT_SIZES))

kernel_name = "hymba_swa_sink_pool_deepnorm"


def t5_causal_bucket(dist):
    max_exact = REL_BUCKETS // 2
    d = jnp.maximum(dist, 0)
    d_f = jnp.maximum(d, 1).astype(jnp.float32)
    large = max_exact + (jnp.log(d_f / max_exact) / math.log(REL_MAX_DIST / max_exact)
                         * (REL_BUCKETS - max_exact)).astype(jnp.int32)
    large = jnp.minimum(large, REL_BUCKETS - 1)
    return jnp.where(d < max_exact, d, large)


def band_geometry():
    qq = jnp.arange(BLOCK)[:, None]
    kk = jnp.arange(2 * BLOCK)[None, :]
    dist = qq + BLOCK - kk
    in_window = (dist >= 0) & (dist < WINDOW)
    return dist, in_window


def layer_norm(x, gain, bias):
    xf = x.astype(jnp.float32)
    mu = jnp.mean(xf, axis=-1, keepdims=True)
    var = jnp.mean(jnp.square(xf - mu), axis=-1, keepdims=True)
    y = (xf - mu) * lax.rsqrt(var + LN_EPS)
    return (y * gain.astype(jnp.float32) + bias.astype(jnp.float32)).astype(x.dtype)


def banded_sink_attention(q, k, v, sinks, bias_hqk, valid):
    B, S, _ = q.shape
    nblk = S // BLOCK
    qb = q.reshape(B, nblk, BLOCK, N_KV_HEADS, KV_GROUP, HEAD_DIM)
    pad = ((0, 0), (BLOCK, 0), (0, 0))
    kp = jnp.pad(k, pad).reshape(B, nblk + 1, BLOCK, N_KV_HEADS, HEAD_DIM)
    vp = jnp.pad(v, pad).reshape(B, nblk + 1, BLOCK, N_KV_HEADS, HEAD_DIM)
    kb = jnp.concatenate([kp[:, :-1], kp[:, 1:]], axis=2)
    vb = jnp.concatenate([vp[:, :-1], vp[:, 1:]], axis=2)
    scale = 1.0 / math.sqrt(HEAD_DIM)
    scores = jnp.einsum('bnqhgd,bnkhd->bnhgqk', qb, kb).astype(jnp.float32) * scale
    scores = scores + bias_hqk[None, None]
    scores = jnp.where(valid[None, :, None, None], scores, -1e30)
    s_sink = sinks.astype(jnp.float32).reshape(N_KV_HEADS, KV_GROUP)[None, None, :, :, None, None]
    m = jnp.maximum(jnp.max(scores, axis=-1, keepdims=True), s_sink)
    e = jnp.exp(scores - m)
    denom = jnp.sum(e, axis=-1, keepdims=True) + jnp.exp(s_sink - m)
    probs = (e / denom).astype(v.dtype)
    out = jnp.einsum('bnhgqk,bnkhd->bnqhgd', probs, vb)
    return out.reshape(B, S, N_HEADS * HEAD_DIM)


def multiscale_pool(u, w_pool, pool_scale):
    B, S, _ = u.shape
    ug = u.reshape(B, S, N_POOL_GROUPS, POOL_GROUP_DIM).astype(jnp.float32)
    cs = jnp.cumsum(ug, axis=1)
    t = jnp.arange(S)
    means = []
    for g, w in enumerate(POOL_WINDOWS):
        c = cs[:, :, g]
        lagged = jnp.pad(c[:, :S - w], ((0, 0), (w, 0), (0, 0)))
        count = jnp.minimum(t + 1, w).astype(jnp.float32)[None, :, None]
        means.append((c - lagged) / count)
    pooled = jnp.stack(means, axis=2)
    diff = (pooled - ug).astype(u.dtype)
    mixed = jnp.einsum('bsgc,gcd->bsgd', diff, w_pool)
    return mixed.reshape(B, S, POOL_WIDTH) * pool_scale


def hybrid_layer(x, p_i, w_in, b_in, w_out, sinks, w_pool, pool_scale, w_ple, w_gate_ple,
                 ln_gain, ln_bias, bias_hqk, valid):
    h = jnp.einsum('bsd,dc->bsc', x, w_in) + b_in
    q, k, v, g_attn, u_pool, g_pool = jnp.split(h, SPLIT_POINTS, axis=-1)
    a = banded_sink_attention(q, k, v, sinks, bias_hqk, valid) * jax.nn.silu(g_attn)
    b = multiscale_pool(u_pool, w_pool, pool_scale) * jax.nn.silu(g_pool)
    mix = jnp.einsum('bsc,cd->bsd', jnp.concatenate([a, b], axis=-1), w_out)
    ple = jax.nn.sigmoid(jnp.einsum('bsd,de->bse', x, w_gate_ple)) * jnp.einsum('bsp,pd->bsd', p_i, w_ple)
    return layer_norm(DEEPNORM_ALPHA * x + mix + ple, ln_gain, ln_bias)


def setup_inputs(seed: int = 0) -> dict:
    key = jax.random.key(seed)
    ks = jax.random.split(key, 14)
    f32 = jnp.float32
    x = jax.random.normal(ks[0], (BATCH, SEQ, D_MODEL), f32)
    p = jax.random.normal(ks[1], (DEPTH, BATCH, SEQ, PLE_DIM), f32)
    w_in = jax.random.normal(ks[2], (DEPTH, D_MODEL, IN_COLS), f32) * D_MODEL ** -0.5
    b_in = jax.random.normal(ks[3], (DEPTH, IN_COLS), f32) * 0.02
    w_out = jax.random.normal(ks[4], (DEPTH, D_MODEL, D_MODEL), f32) * (D_MODEL ** -0.5 * DEEPNORM_BETA)
    attn_sinks = jax.random.normal(ks[5], (DEPTH, N_HEADS), f32) * 0.5
    rel_bias = jax.random.normal(ks[6], (REL_BUCKETS, N_HEADS), f32) * 0.1
    w_pool = jax.random.normal(ks[7], (DEPTH, N_POOL_GROUPS, POOL_GROUP_DIM, POOL_GROUP_DIM), f32) * POOL_GROUP_DIM ** -0.5
    pool_scale = 1.0 + 0.1 * jax.random.normal(ks[8], (DEPTH, POOL_WIDTH), f32)
    w_ple = jax.random.normal(ks[9], (DEPTH, PLE_DIM, D_MODEL), f32) * PLE_DIM ** -0.5
    w_gate_ple = jax.random.normal(ks[10], (DEPTH, D_MODEL, D_MODEL), f32) * D_MODEL ** -0.5
    ln_gain = 1.0 + 0.02 * jax.random.normal(ks[11], (DEPTH, D_MODEL), f32)
    ln_bias = 0.02 * jax.random.normal(ks[12], (DEPTH, D_MODEL), f32)
    return {"x": x, "p": p, "w_in": w_in, "b_in": b_in, "w_out": w_out,
            "attn_sinks": attn_sinks, "rel_bias": rel_bias, "w_pool": w_pool,
            "pool_scale": pool_scale, "w_ple": w_ple, "w_gate_ple": w_gate_ple,
            "ln_gain": ln_gain, "ln_bias": ln_bias}


def reference(x, p, w_in, b_in, w_out, attn_sinks, rel_bias, w_pool, pool_scale, w_ple,
              w_gate_ple, ln_gain, ln_bias):
    S = x.shape[1]
    nblk = S // BLOCK
    dist, in_window = band_geometry()
    bias_hqk = jnp.transpose(rel_bias[t5_causal_bucket(dist)], (2, 0, 1)).astype(jnp.float32)
    bias_hqk = bias_hqk.reshape(N_KV_HEADS, KV_GROUP, BLOCK, 2 * BLOCK)
    k_pos = (jnp.arange(nblk) * BLOCK - BLOCK)[:, None, None] + jnp.arange(2 * BLOCK)[None, None, :]
    valid = in_window[None] & (k_pos >= 0)
    for i in range(DEPTH):
        x = hybrid_layer(x, p[i], w_in[i], b_in[i], w_out[i], attn_sinks[i], w_pool[i],
                         pool_scale[i], w_ple[i], w_gate_ple[i], ln_gain[i], ln_bias[i],
                         bias_hqk, valid)
    return x
```

```python
import math
from contextlib import ExitStack

import numpy as np
import concourse.bass as bass
import concourse.mybir as mybir
from concourse.bass_utils import run_bass_kernel_spmd

F32 = mybir.dt.float32
BF16 = mybir.dt.bfloat16
U8 = mybir.dt.uint8
I32 = mybir.dt.int32
AF = mybir.ActivationFunctionType
ALU = mybir.AluOpType
ESZ = {F32: 4, BF16: 2, U8: 1, I32: 4}

D = 2048
DC = 16
DEPTH = 4
B, S = 2, 4096
NCORES = 8
CHUNK = 1024
HALO = 512
T = CHUNK + HALO
NT = T // 128
G = 512
GT = 4
NG = T // G
ALPHA = (2.0 * DEPTH) ** 0.25
LN_EPS = 1e-5
NCOL = 82
NA = 35
CH_KD = 0
CH_V = 2
CH_Q = 3
CH_GA = 11
CH_U = 19
CH_GP = 27
POOL_WINDOWS = (2, 4, 8, 16)

NSLOT = 3
WLOOK = 2
USE_WLOOK = False
SYNC_SAME_ENGINE_ALL = True
NF = 8


class _Op:
    __slots__ = ("eng", "fn", "is_dma", "key", "idx", "cnt", "signal", "waits", "semval")

    def __init__(self, eng, fn, is_dma=False, key=None):
        self.eng = eng
        self.fn = fn
        self.is_dma = is_dma
        self.key = key
        self.idx = -1
        self.cnt = 0
        self.signal = False
        self.waits = []
        self.semval = 0


class Sched:
    ENGS = ("pe", "act", "dve", "pool", "sp")

    def __init__(self):
        self.ops = {e: [] for e in self.ENGS}
        self.recs = {}
        self.waited = {e: {} for e in self.ENGS}
        self.dma_cnt = {}
        self.dma_keys = []

    @staticmethod
    def interval(ap):
        name = ap.tensor.name
        if name.startswith("arena"):
            space = "sb"
        elif name.startswith("psum"):
            space = "ps"
        else:
            return None
        dims = ap.ap
        pstep = dims[0][0]
        off = ap.offset % pstep
        ext = 1
        for st, cn in dims[1:]:
            ext += (cn - 1) * st
        esz = ESZ[ap.dtype]
        if space == "ps":
            b0 = (off * esz) // 2048
            b1 = ((off + ext) * esz - 1) // 2048
            return (space, b0 * 2048, (b1 + 1) * 2048)
        return (space, off * esz, (off + ext) * esz)

    def _need(self, op, prod, raw):
        if prod is op:
            return
        if prod.is_dma:
            k = ("dma", prod.key)
            if self.waited[op.eng].get(k, 0) >= prod.cnt:
                return
            self.waited[op.eng][k] = prod.cnt
            op.waits.append(prod)
            prod.signal = True
            return
        if prod.eng == op.eng and not op.is_dma:
            if op.eng == "pe":
                return
            if not raw and not SYNC_SAME_ENGINE_ALL:
                return
        k = prod.eng
        if self.waited[op.eng].get(k, -1) >= prod.idx:
            return
        self.waited[op.eng][k] = prod.idx
        op.waits.append(prod)
        prod.signal = True

    def _access(self, op, ap, is_write):
        iv = self.interval(ap)
        if iv is None:
            return
        space, s, e = iv
        recs = self.recs.setdefault(space, [])
        keep = []
        for r in recs:
            rs, re_, rop, rw = r
            if rs < e and s < re_:
                if is_write:
                    self._need(op, rop, raw=False)
                    if s <= rs and re_ <= e:
                        continue
                elif rw:
                    self._need(op, rop, raw=True)
                elif rop.eng == op.eng and rs == s and re_ == e and not rop.is_dma and not op.is_dma:
                    continue
            keep.append(r)
        keep.append((s, e, op, is_write))
        self.recs[space] = keep

    def add(self, eng, fn, reads=(), writes=(), dma_key=None):
        op = _Op(eng, fn, is_dma=dma_key is not None, key=dma_key)
        lst = self.ops[eng]
        op.idx = len(lst)
        if op.is_dma:
            if dma_key not in self.dma_cnt:
                self.dma_cnt[dma_key] = 0
                self.dma_keys.append(dma_key)
            self.dma_cnt[dma_key] += 1
            op.cnt = self.dma_cnt[dma_key]
        for ap in reads:
            self._access(op, ap, False)
        for ap in writes:
            self._access(op, ap, True)
        lst.append(op)
        return op

    def emit(self, nc, stack, final_dma_keys=()):
        sems = {e: stack.enter_context(nc.semaphore("s_" + e)) for e in self.ENGS}
        dsems = {k: stack.enter_context(nc.semaphore("d_%d" % i)) for i, k in enumerate(self.dma_keys)}
        for e in self.ENGS:
            c = 0
            for op in self.ops[e]:
                if op.signal and not op.is_dma:
                    c += 1
                    op.semval = c
        block = stack.enter_context(nc.Block())
        handles = {"pe": block.tensor, "act": block.scalar, "dve": block.vector,
                   "pool": block.gpsimd, "sp": block.sync}

        def make(ename):
            def body(eng):
                for op in self.ops[ename]:
                    for p in op.waits:
                        if p.is_dma:
                            eng.wait_ge(dsems[p.key], 16 * p.cnt)
                        else:
                            eng.wait_ge(sems[p.eng], p.semval)
                    inst = op.fn(eng)
                    if op.is_dma:
                        inst.then_inc(dsems[op.key], 16)
                    elif op.signal:
                        inst.then_inc(sems[ename], 1)
                if ename == "sp":
                    for k in final_dma_keys:
                        if k in dsems:
                            eng.wait_ge(dsems[k], 16 * self.dma_cnt[k])
            return body

        for e in self.ENGS:
            handles[e](make(e))


class Arena:
    def __init__(self, tensor, size):
        self.t = tensor
        self.size = size
        self.top = 0

    def alloc(self, nbytes):
        off = (self.top + 31) // 32 * 32
        self.top = off + nbytes
        assert self.top <= self.size, ("SBUF arena overflow", self.top, self.size)
        return off

    def view(self, off, shape, dtype):
        n = int(np.prod(shape)) * ESZ[dtype]
        v = self.t[:, off:off + n].bitcast(dtype)
        if len(shape) == 1:
            return v
        names = "abcde"[:len(shape)]
        pat = "p (" + " ".join(names) + ") -> p " + " ".join(names)
        kw = {names[i]: shape[i] for i in range(len(shape) - 1)}
        return v.rearrange(pat, **kw)

    def new(self, shape, dtype):
        n = int(np.prod(shape)) * ESZ[dtype]
        off = self.alloc(n)
        return self.view(off, shape, dtype), off


def build_program(n_layers, ple_dtype=F32, wlist=None, record=False):
    L = n_layers
    nc = bass.Bass("TRN2", target_bir_lowering=False)
    dr = {}

    def dram(name, shape, kind="ExternalInput"):
        dr[name] = nc.dram_tensor(name, list(shape), F32, kind=kind).ap()
        return dr[name]

    xT_d = dram("xT", [DC, 128, T])
    pT_d = dram("pT", [L, 128, 2, T])
    wA_d = dram("wA", [L, NA, 128, 2048])
    wP_d = dram("wP", [L, 4, 128, 512])
    wG_d = dram("wG", [L, DC, 128, 2048])
    wO_d = dram("wO", [L, DC, 128, 2048])
    wE_d = dram("wE", [L, DC, 128, 256])
    cols_d = dram("cols", [128, L * NCOL])
    rows_d = dram("rows", [L, 1152])
    biasT_d = dram("biasT", [8, 128, 512])
    mask_d = dram("mask01", [128, 512])
    corr_d = dram("corr", [128, 64])
    flag_d = dram("flag", [128, 8])
    ident_d = dram("ident", [128, 128])
    out_d = dram("outT", [DC, 128, CHUNK], kind="ExternalOutput")

    stack = ExitStack()
    ARENA_BYTES = 211456
    arena_t = stack.enter_context(nc.sbuf_tensor("arena", [128, ARENA_BYTES], U8))
    psum_t = stack.enter_context(nc.psum_tensor("psum", [128, 8, 512], F32))
    A = Arena(arena_t, ARENA_BYTES)
    S_ = Sched()

    def PS(b):
        return psum_t[:, b, :]

    X2, x2_off = A.new([DC, NG, 2, G], BF16)
    WS, _ = A.new([NSLOT, 2048], BF16)
    WPL, _ = A.new([2, 512], BF16)
    WPE, _ = A.new([2, 256], BF16)
    KD, _ = A.new([2, 5, 128], BF16)
    VP, _ = A.new([5, 2, 192], BF16)
    UCY, _ = A.new([8, 16], F32)
    CORR, _ = A.new([4, 16], F32)
    ET, _ = A.new([8, 512], BF16)
    COLS, _ = A.new([L * NCOL], F32)
    ESK, _ = A.new([L * 8], F32)
    PSH, _ = A.new([L * 8], F32)
    ROWB, _ = A.new([1152], BF16)
    R1, _ = A.new([128], BF16)
    ONES, _ = A.new([128], BF16)
    IDENT, _ = A.new([128], BF16)
    OPAD, _ = A.new([192], BF16)
    OPADF, _ = A.new([192], BF16)
    FLAG, _ = A.new([8], F32)
    AB, _ = A.new([DC, G], BF16)
    PTB, _ = A.new([1, 2, G], BF16)
    FT, _ = A.new([NF, G], F32)
    ubase = A.alloc(0)
    UB, _ = A.new([4, 528], F32)
    QAB, _ = A.new([2, GT, 2, 128], BF16)
    PT2, _ = A.new([2, G], BF16)
    DF, _ = A.new([2, 2, G], BF16)
    top_ab = A.top
    A.top = ubase
    PLE, _ = A.new([DC, G], ple_dtype)
    YH, _ = A.new([2, G], BF16)
    YS, _ = A.new([2, G], BF16)
    A.top = max(A.top, top_ab)

    def XH(dc, g):
        return X2[:, dc, g, 0, :]

    def XL(dc, g):
        return X2[:, dc, g, 1, :]

    def YV(dc, g):
        off = x2_off + ((dc * NG + g) * 2) * G * 2
        return A.view(off, [G], F32)

    def F(k):
        return FT[:, k, :]

    def Fi(k):
        return FT[:, k, :].bitcast(I32)

    def dma(eng, out, in_, key):
        S_.add(eng, lambda e: e.dma_start(out=out, in_=in_), reads=[in_], writes=[out], dma_key=key)

    def mm(out, pairs):
        n = len(pairs)
        rd = []
        for a, b in pairs:
            rd.append(a)
            rd.append(b)

        def fn(e):
            inst = None
            for i, (lhsT, rhs) in enumerate(pairs):
                inst = e.matmul(out, lhsT=lhsT, rhs=rhs, start=(i == 0), stop=(i == n - 1))
            return inst
        S_.add("pe", fn, reads=rd, writes=[out])

    def act(out, in_, func, bias=None, scale=1.0, eng="act"):
        rd = [in_]
        kw = {}
        if bias is not None:
            kw["bias"] = bias
            if not isinstance(bias, (int, float)):
                rd.append(bias)
        if not isinstance(scale, (int, float)):
            rd.append(scale)
        S_.add(eng, lambda e: e.activation(out=out, in_=in_, func=func, scale=scale, **kw),
               reads=rd, writes=[out])

    def tt(out, in0, in1, op, eng="dve"):
        S_.add(eng, lambda e: e.tensor_tensor(out=out, in0=in0, in1=in1, op=op), reads=[in0, in1], writes=[out])

    def ts(out, in0, s1, op0, s2=None, op1=None, eng="dve"):
        rd = [in0]
        if not isinstance(s1, (int, float)):
            rd.append(s1)
        if s2 is not None and not isinstance(s2, (int, float)):
            rd.append(s2)

        def fn(e):
            if op1 is None:
                return e.tensor_scalar(out=out, in0=in0, scalar1=s1, scalar2=None, op0=op0)
            return e.tensor_scalar(out=out, in0=in0, scalar1=s1, scalar2=s2, op0=op0, op1=op1)
        S_.add(eng, fn, reads=rd, writes=[out])

    def stt(out, in0, scalar, in1, op0, op1, eng="dve"):
        rd = [in0, in1]
        if not isinstance(scalar, (int, float)):
            rd.append(scalar)
        S_.add(eng, lambda e: e.scalar_tensor_tensor(out=out, in0=in0, scalar=scalar, in1=in1, op0=op0, op1=op1),
               reads=rd, writes=[out])

    def cp(out, in_, eng="dve"):
        if eng == "act":
            S_.add(eng, lambda e: e.activation(out=out, in_=in_, func=AF.Copy), reads=[in_], writes=[out])
        else:
            S_.add(eng, lambda e: e.tensor_copy(out=out, in_=in_), reads=[in_], writes=[out])

    def recip(out, in_):
        S_.add("dve", lambda e: e.reciprocal(out=out, in_=in_), reads=[in_], writes=[out])

    def memset(ap, val, eng="dve"):
        S_.add(eng, lambda e: e.memset(ap, val), writes=[ap])

    wstate = {"n": 0, "emitted": 0}
    wrec = []

    def wload(src):
        n = wstate["n"]
        wstate["n"] += 1
        s = n % NSLOT
        if wlist is None:
            wrec.append(src)
            dma("pool", WS[:, s, :], dr[src[0]][src[1], src[2]], ("w", s))
        else:
            upto = min(n + 1 + WLOOK, len(wlist))
            while wstate["emitted"] < upto:
                m = wstate["emitted"]
                wsrc = wlist[m]
                dma("pool", WS[:, m % NSLOT, :], dr[wsrc[0]][wsrc[1], wsrc[2]], ("w", m % NSLOT))
                wstate["emitted"] += 1
        return WS[:, s, :].rearrange("p (a b) -> p a b", a=16)

    small = {"wp": 0, "we": 0}

    dma("sp", COLS, cols_d, "c_cols")
    dma("sp", FLAG, flag_d, "c_flag")
    memset(ONES, 1.0)
    memset(OPAD, 0.0)
    memset(OPAD[:, 64:128], 1.0)
    memset(R1, 0.0)
    memset(R1[0:1, :], 1.0)
    memset(ROWB, 0.0)
    memset(VP.rearrange("p a b c -> p (a b c)"), 0.0)
    ts(OPADF, OPAD, FLAG[:, 0:1], ALU.mult)
    for l in range(L):
        c0 = l * NCOL
        act(ESK[:, l * 8:(l + 1) * 8], COLS[:, c0 + 34:c0 + 42], AF.Exp)
        ts(PSH[:, l * 8:(l + 1) * 8], COLS[:, c0 + 26:c0 + 34], 0.5, ALU.mult)
    def xload(g_, dc_, nbuf=2, base=6):
        k_ = base + (dc_ % nbuf)
        fa = F(k_)
        dma("sp", fa, xT_d[dc_][:, g_ * G:(g_ + 1) * G], ("c_x", k_))
        act(XH(dc_, g_), fa, AF.Copy)
        tt(XL(dc_, g_), fa, XH(dc_, g_), ALU.subtract)

    for dc in range(DC):
        xload(0, dc, nbuf=8, base=0)
    dma("sp", CORR.rearrange("p a b -> p (a b)"), corr_d, "c_corr")
    dma("sp", F(3), mask_d, "c_mask")
    ts(F(3), F(3), 240000.0, ALU.mult, -240000.0, ALU.add)
    for j in range(8):
        fa = F(4 + (j % 2))
        dma("sp", fa, biasT_d[j], ("c_bias", j % 2))
        stt(ET[:, j, :], fa, 8.0, F(3), ALU.mult, ALU.add)
    dma("sp", F(2)[:, 0:128], ident_d, "c_ident")
    cp(IDENT, F(2)[:, 0:128])

    deferred = []

    def drain(n):
        while deferred and n > 0:
            deferred.pop(0)()
            n -= 1

    for g_ in range(1, NG):
        for dc_ in range(DC):
            deferred.append(lambda g_=g_, dc_=dc_: xload(g_, dc_))

    def layer_group(l, g, last_layer):
        c0 = l * NCOL
        i_full = l + 5 - L
        if g == 0:
            i_f = min(max(i_full, 0), GT)
            i_k = max(i_f - 1, 0)
        else:
            i_f = i_k = 0
        carry_only = i_f >= GT
        cf, ck = 128 * i_f, 128 * i_k

        def col(k):
            return COLS[:, c0 + k:c0 + k + 1]

        def c(ap):
            return ap[:, cf:]

        Xg = [XH(dc, g) for dc in range(DC)]

        def hchunk(cid, bank, cc=None):
            cc = cf if cc is None else cc
            W = wload(("wA", l, cid))
            mm(PS(bank)[:, cc:], [(W[:, dc, :], Xg[dc][:, cc:]) for dc in range(DC)])

        hchunk(CH_KD, 0, ck)
        act(KD[:, 0, 1 + i_k:5, :].rearrange("p a b -> p (a b)"), PS(0)[:, ck:], AF.Identity, bias=col(8))
        W = wload(("wA", l, CH_V))
        for i in range(i_k, GT):
            pairs = [(Xg[dc][:, i * 128:(i + 1) * 128], W[:, dc, :]) for dc in range(DC)]
            pairs.append((R1, ROWB[:, 0:128]))
            mm(PS(2)[:, i * 128:(i + 1) * 128], pairs)
        for i in range(i_k, GT):
            cp(VP[:, 1 + i, :, 64:128], PS(2)[:, i * 128:(i + 1) * 128].rearrange("p (a b) -> p a b", a=2), eng="act")

        if not carry_only:
            for b_ in range(2):
                memset(QAB[64:128, b_, :, 0, :], 0.0)
                memset(QAB[0:64, b_, :, 1, :], 0.0)
            hb = {"n": 0}
            blocks = list(range(i_f, GT))
            nb = len(blocks)
            seq = []
            for t_ in range(nb + 2):
                if 2 <= t_ and t_ - 2 < nb:
                    seq.append(("D", blocks[t_ - 2]))
                if t_ < nb:
                    seq.append(("S", blocks[t_]))
            done_d = [x[1] for x in seq if x[0] == "D"]
            for b_ in blocks:
                if b_ not in done_d:
                    seq.append(("D", b_))
            k1 = 0
            while k1 < len(seq) and seq[k1][0] == "S":
                k1 += 1
            rest = seq[k1:]
            st1, st3, st5 = seq[:k1], rest[:len(rest) // 2], rest[len(rest) // 2:]

            def next_hbank():
                b_ = (0, 1, 4)[hb["n"] % 3]
                hb["n"] += 1
                return b_

            def attn_chunk(j, qa, qb, gbank):
                den, num = PS(5), PS(6 + (j % 2))
                sb = [2, 3]

                def s_block(i):
                    sp_ = PS(sb[i % 2])
                    skip_prev = (g == 0 and i == 0)
                    items = [(sp_, IDENT, ET[:, j, :])]
                    qsrc = qa[:, i, :, :].rearrange("p a b -> p (a b)")
                    for pc in range(2):
                        if skip_prev and pc == 0:
                            continue
                        items.append((sp_[:, pc * 256:(pc + 1) * 256], KD[:, 0, i + pc, :], qsrc))
                    rd = []
                    for o_, a_, b_ in items:
                        rd += [a_, b_]
                    n_ = len(items)

                    def fn(e, items=items, n_=n_):
                        inst = None
                        for ii, (o_, a_, b_) in enumerate(items):
                            inst = e.matmul(o_, lhsT=a_, rhs=b_, start=(ii == 0), stop=(ii == n_ - 1))
                        return inst
                    S_.add("pe", fn, reads=rd, writes=[sp_])
                    act(PT2[:, i % 2, :], sp_, AF.Exp, scale=0.125)

                def dn_block(i):
                    p2 = PT2[:, i % 2, :]
                    dpairs, npairs = [], []
                    for hd in range(2):
                        for pc in range(2):
                            if g == 0 and i == 0 and pc == 0:
                                continue
                            rhs = p2[:, (pc * 2 + hd) * 128:(pc * 2 + hd + 1) * 128]
                            op_ = OPADF if (g == 1 and i == 0 and pc == 0) else OPAD
                            lo_ = 64 if hd == 0 else 0
                            dpairs.append((op_[:, lo_:lo_ + 128], rhs))
                            npairs.append((VP[:, i + pc, hd, lo_:lo_ + 128], rhs))
                    mm(den[:, i * 128:(i + 1) * 128], dpairs)
                    mm(num[:, i * 128:(i + 1) * 128], npairs)

                def gate_pre():
                    k2 = 2 * (j % 2)
                    f0, f1 = c(F(k2)), c(F(k2 + 1))
                    act(f0, c(PS(gbank)), AF.Identity, bias=col(10 + j))
                    act(f1, f0, AF.Tanh, scale=0.5)

                def gate_chain():
                    k2 = 2 * (j % 2)
                    f0, f1 = c(F(k2)), c(F(k2 + 1))
                    stt(f1, f1, 1.0, f0, ALU.add, ALU.mult)
                    act(f0, c(den), AF.Identity, bias=ESK[:, l * 8 + j:l * 8 + j + 1])
                    recip(f0, f0)
                    tt(f0, f0, f1, ALU.mult)
                    stt(c(AB[:, j, :]), c(num), 0.5, f0, ALU.mult, ALU.mult)

                def run(stage):
                    for kind, i in stage:
                        (s_block if kind == "S" else dn_block)(i)

                return run, gate_chain, gate_pre

            pend = None
            for j in range(8):
                qbank = next_hbank()
                hchunk(CH_Q + j, qbank)
                qa = QAB[:, j % 2, :, :, :]
                qb = qa
                act(QAB[0:64, j % 2, i_f:, 0, :], PS(qbank)[0:64, cf:].rearrange("p (a b) -> p a b", b=128),
                    AF.Identity, bias=COLS[0:64, c0 + j:c0 + j + 1])
                act(QAB[64:128, j % 2, i_f:, 1, :], PS(qbank)[64:128, cf:].rearrange("p (a b) -> p a b", b=128),
                    AF.Identity, bias=COLS[64:128, c0 + j:c0 + j + 1])
                if pend is not None:
                    pend[0](st3)
                gbank = next_hbank()
                hchunk(CH_GA + j, gbank)
                if pend is not None:
                    pend[0](st5)
                nxt = attn_chunk(j, qa, qb, gbank)
                nxt[2]()
                nxt[0](st1)
                if pend is not None:
                    pend[1]()
                drain(2)
                pend = nxt
            pend[0](st3)
            pend[0](st5)
            pend[1]()
            drain(2)

        ubs = {"n": 0}

        def pool_u(pg):
            w = POOL_WINDOWS[pg]
            nsteps = {2: 1, 4: 2, 8: 3, 16: 4}[w]
            for cc in range(2):
                cu = 2 * pg + cc
                bank = cu % 2
                hchunk(CH_U + cu, bank, ck)
                ua = UB[:, ubs["n"] % 2, :]
                ubs["n"] += 1
                tb = [UB[:, 2, :], UB[:, 3, :]]
                act(ua[:, 16 + ck:], PS(bank)[:, ck:], AF.Identity, bias=col(74 + cu))
                if g >= 1:
                    cp(ua[:, 0:16], UCY[:, cu, :], eng="dve")
                else:
                    memset(ua[:, 0:16], 0.0)
                if g < NG - 1:
                    if g == 0:
                        ts(UCY[:, cu, :], ua[:, 512:528], FLAG[:, 0:1], ALU.mult)
                    else:
                        cp(UCY[:, cu, :], ua[:, 512:528], eng="dve")
                if carry_only:
                    continue
                lo = 0 if g >= 1 else 16 + ck
                src = ua
                sh = 1
                for st_ in range(nsteps):
                    dst = tb[st_ % 2]
                    lo2 = lo + sh
                    tt(dst[:, lo2:], src[:, lo2:], src[:, lo2 - sh:528 - sh], ALU.add)
                    src, lo, sh = dst, lo2, sh * 2
                if g == 1:
                    tt(src[:, 16:32], src[:, 16:32], CORR[:, pg, :], ALU.mult)
                stt(c(DF[:, pg % 2, cc, :]), src[:, 16 + cf:], 1.0 / w, ua[:, 16 + cf:], ALU.mult, ALU.subtract)

        def pool_mix(pg):
            s_ = small["wp"] % 2
            small["wp"] += 1
            dma("pool", WPL[:, s_, :], wP_d[l, pg], ("wp", s_))
            Wp = WPL[:, s_, :].rearrange("p (a b) -> p a b", a=2)
            gbanks = []
            for dd in range(2):
                jb = 2 * pg + dd
                hchunk(CH_GP + jb, 6 + dd)
            for dd in range(2):
                mbank = 4 + dd
                mm(c(PS(mbank)), [(Wp[:, kc, dd * 128:(dd + 1) * 128], c(DF[:, pg % 2, kc, :])) for kc in range(2)])
                jb = 2 * pg + dd
                gbank = 6 + dd
                k2 = 2 * (jb % 2)
                f0, f1 = c(F(k2)), c(F(k2 + 1))
                act(f0, c(PS(gbank)), AF.Identity, bias=col(18 + jb))
                act(f1, f0, AF.Tanh, scale=0.5)
                stt(f1, f1, 1.0, f0, ALU.add, ALU.mult)
                stt(c(AB[:, 8 + jb, :]), c(PS(mbank)), PSH[:, l * 8 + jb:l * 8 + jb + 1], f1, ALU.mult, ALU.mult)
                drain(1)

        for pg in range(5):
            if pg < 4:
                pool_u(pg)
            if pg > 0 and not carry_only:
                pool_mix(pg - 1)

        if g < NG - 1:
            cp(KD[:, 0, 0, :], KD[:, 0, 4, :], eng="dve")
            if g == 0:
                ts(VP[:, 0, :, :], VP[:, 4, :, :], FLAG[:, 0:1], ALU.mult)
            else:
                cp(VP[:, 0, :, :], VP[:, 4, :, :], eng="dve")
        if carry_only:
            return

        drain(10 ** 6)
        pb = 0
        dma("pool", PTB[:, pb, :, :], pT_d[l][:, :, g * G:(g + 1) * G], ("pt", pb))
        for dch in range(DC):
            W = wload(("wG", l, dch))
            gb = dch % 2
            mm(c(PS(gb)), [(W[:, dc, :], c(Xg[dc])) for dc in range(DC)])
            s_ = small["we"] % 2
            small["we"] += 1
            dma("pool", WPE[:, s_, :], wE_d[l, dch], ("we", s_))
            We = WPE[:, s_, :].rearrange("p (a b) -> p a b", a=2)
            eb = 2 + dch % 2
            mm(c(PS(eb)), [(We[:, pc, :], c(PTB[:, pb, pc, :])) for pc in range(2)])
            f0 = c(F(dch % 2))
            act(f0, c(PS(gb)), AF.Tanh, scale=0.5)
            stt(c(PLE[:, dch, :]), f0, 1.0, c(PS(eb)), ALU.add, ALU.mult)

        stats_pend = []

        def stats_mm(dch, yh, ys):
            S_.add("pe", (lambda e: e.matmul(c(PS(6)), lhsT=ONES, rhs=yh, start=(dch == 0), stop=(dch == DC - 1))),
                   reads=[ONES, yh], writes=[c(PS(6))])
            S_.add("pe", (lambda e: e.matmul(c(PS(7)), lhsT=ONES, rhs=ys, start=(dch == 0), stop=(dch == DC - 1))),
                   reads=[ONES, ys], writes=[c(PS(7))])

        for dch in range(DC):
            W = wload(("wO", l, dch))
            mb = 4 + dch % 2
            mm(c(PS(mb)), [(W[:, cc, :], c(AB[:, cc, :])) for cc in range(DC)])
            if len(stats_pend) >= 1:
                stats_mm(*stats_pend.pop(0))
            f0 = c(F(2 + dch % 2))
            tt(f0, c(XH(dch, g)), c(XL(dch, g)), ALU.add)
            stt(f0, f0, ALPHA, c(PS(mb)), ALU.mult, ALU.add)
            y = c(YV(dch, g))
            stt(y, c(PLE[:, dch, :]), 0.5, f0, ALU.mult, ALU.add)
            yh, ys = c(YH[:, dch % 2, :]), c(YS[:, dch % 2, :])
            act(yh, y, AF.Copy)
            act(ys, y, AF.Square)
            stats_pend.append((dch, yh, ys))
        while stats_pend:
            stats_mm(*stats_pend.pop(0))

        mean, var, r, tmp = c(F(7)), c(F(4)), c(F(6)), c(F(5))
        ts(mean, c(PS(6)), 1.0 / D, ALU.mult)
        tt(tmp, mean, mean, ALU.mult)
        stt(var, c(PS(7)), 1.0 / D, tmp, ALU.mult, ALU.subtract)
        ts(var, var, LN_EPS, ALU.add)
        ts(r, var, 0.25, ALU.mult, 1.0, ALU.add)
        recip(r, r)
        for _ in range(5):
            tt(tmp, r, r, ALU.mult)
            tt(tmp, tmp, var, ALU.mult)
            ts(tmp, tmp, -0.5, ALU.mult, 1.5, ALU.add)
            tt(r, r, tmp, ALU.mult)
        tt(mean, mean, r, ALU.mult)
        def ln_a(dch):
            y = c(YV(dch, g))
            tt(y, y, r, ALU.mult)
            tt(y, y, mean, ALU.subtract)

        def ln_b(dch):
            y = c(YV(dch, g))
            f1 = c(F(4 + dch % 2))
            act(f1, y, AF.Identity, bias=col(58 + dch), scale=col(42 + dch))
            if last_layer:
                dma("sp", out_d[dch][:, (g - 1) * G + cf:g * G], f1, ("out", dch % 2))
            else:
                act(c(XH(dch, g)), f1, AF.Copy)
                tt(c(XL(dch, g)), f1, c(XH(dch, g)), ALU.subtract)

        for k in range(DC + 2):
            def item(k=k):
                if k < DC:
                    ln_a(k)
                if k >= 2:
                    ln_b(k - 2)
            deferred.append(item)

    for l in range(L):
        dma("pool", ROWB[0:1, :], rows_d[l:l + 1, :], "rows")
        for g in range(NG):
            layer_group(l, g, l == L - 1)

    drain(10 ** 6)
    if record:
        stack.close()
        return wrec
    S_.emit(nc, stack, final_dma_keys=[("out", 0), ("out", 1)])
    print("arena top", A.top, "of", ARENA_BYTES, "n_ops", {e: len(v) for e, v in S_.ops.items()})
    stack.close()
    return nc


def _t5_bucket(dist):
    max_exact = 16
    d = np.maximum(dist, 0)
    d_f = np.maximum(d, 1).astype(np.float64)
    large = max_exact + (np.log(d_f / max_exact) / math.log(128 / max_exact) * (32 - max_exact)).astype(np.int64)
    large = np.minimum(large, 31)
    return np.where(d < max_exact, d, large)


def _const_tables():
    k = np.arange(128)[:, None]
    q = np.arange(128)[None, :]
    dist = np.zeros((128, 4, 128), np.int64)
    mask = np.zeros((128, 4, 128), np.float32)
    for hd in range(2):
        for pc in range(2):
            d = q + 128 - (k + 128 * pc)
            sl = pc * 2 + hd
            dist[:, sl, :] = d
            mask[:, sl, :] = ((d >= 0) & (d < 128)).astype(np.float32)
    bucket = _t5_bucket(np.clip(dist, 0, 255))
    j = np.arange(128)[:, None]
    t = np.arange(128)[None, :]
    ptm = np.zeros((128, 8, 128), np.float32)
    ptf = np.zeros((128, 4, 128), np.float32)
    for pg, w in enumerate(POOL_WINDOWS):
        ptm[:, pg, :] = np.where(j - 128 >= t - w + 1, 1.0 / w, 0.0)
        cur = np.where((j <= t) & (j >= t - w + 1), 1.0 / w, 0.0) - (j == t)
        ptm[:, 4 + pg, :] = cur
        cnt = np.minimum(t + 1, w).astype(np.float64)
        ptf[:, pg, :] = (np.where((j <= t) & (j >= t - w + 1), 1.0 / cnt, 0.0) - (j == t)).astype(np.float32)
    corr = np.ones((4, 16), np.float64)
    for pg, w in enumerate(POOL_WINDOWS):
        tt_ = np.arange(16)
        corr[pg] = w / np.minimum(tt_ + 1, w)
    corr_first = np.broadcast_to(corr.reshape(1, 64), (128, 64)).astype(np.float32).copy()
    return bucket, mask.reshape(128, 512), corr_first


def _layer_weights(inp, l):
    w_in = np.asarray(inp["w_in"][l], np.float32)
    b_in = np.asarray(inp["b_in"][l], np.float32)

    def chunk(cols):
        w = w_in[:, cols]
        return w.reshape(DC, 128, 128).transpose(1, 0, 2).reshape(128, 2048)

    wA = np.empty((NA, 128, 2048), np.float32)
    wA[CH_KD] = chunk(np.arange(1024, 1152))
    wA[CH_KD + 1] = wA[CH_KD]

    def hpair(j):
        return np.concatenate([np.arange(64 * j, 64 * j + 64), np.arange(512 + 64 * j, 512 + 64 * j + 64)])
    wA[CH_V] = chunk(np.arange(1152, 1280))
    for j in range(8):
        wA[CH_Q + j] = chunk(hpair(j))
        wA[CH_GA + j] = chunk(1280 + hpair(j))
        wA[CH_U + j] = chunk(np.arange(2304 + 128 * j, 2304 + 128 * j + 128))
        wA[CH_GP + j] = chunk(np.arange(3328 + 128 * j, 3328 + 128 * j + 128))
    wp = np.asarray(inp["w_pool"][l], np.float32)
    wP = wp.reshape(4, 2, 128, 256).transpose(0, 2, 1, 3).reshape(4, 128, 512)

    def ochunks(w, kc):
        r = w.reshape(kc, 128, DC, 128).transpose(2, 1, 0, 3)
        return np.ascontiguousarray(r).reshape(DC, 128, kc * 128)

    wG = ochunks(np.asarray(inp["w_gate_ple"][l], np.float32), 16)
    rperm = np.concatenate([hpair(j) for j in range(8)] + [np.arange(1024, 2048)])
    wO = ochunks(np.asarray(inp["w_out"][l], np.float32)[rperm], 16)
    wE = ochunks(np.asarray(inp["w_ple"][l], np.float32), 2)
    cols = np.empty((128, NCOL), np.float32)
    p = np.arange(128)
    for j in range(8):
        cols[:, j] = b_in[hpair(j)]
        cols[:, 10 + j] = b_in[1280 + hpair(j)]
        cols[:, 18 + j] = b_in[3328 + 128 * j + p]
        cols[:, 26 + j] = np.asarray(inp["pool_scale"][l], np.float32)[128 * j + p]
        sk = np.asarray(inp["attn_sinks"][l], np.float32)
        cols[:, 34 + j] = np.where(p < 64, sk[j], sk[8 + j])
    cols[:, 8] = b_in[1024 + p]
    cols[:, 9] = b_in[1024 + p]
    for dch in range(DC):
        cols[:, 42 + dch] = np.asarray(inp["ln_gain"][l], np.float32)[128 * dch + p]
        cols[:, 58 + dch] = np.asarray(inp["ln_bias"][l], np.float32)[128 * dch + p]
    for j in range(8):
        cols[:, 74 + j] = b_in[2304 + 128 * j + p]
    rows = np.concatenate([b_in[1152:1280], b_in[2304:3328]]).astype(np.float32)
    return wA, wP, wG, wO, wE, cols, rows


_PROG_CACHE = {}


def _get_prog(n_layers):
    if n_layers not in _PROG_CACHE:
        if USE_WLOOK:
            wl = build_program(n_layers, record=True)
            _PROG_CACHE[n_layers] = build_program(n_layers, wlist=wl)
        else:
            _PROG_CACHE[n_layers] = build_program(n_layers)
    return _PROG_CACHE[n_layers]


def _run_layers(x_full, inp, layers):
    nl = len(layers)
    nc = _get_prog(nl)
    bucket, mask01, corr_first = _const_tables()
    rel_bias = np.asarray(inp["rel_bias"], np.float32)
    gathered = rel_bias[bucket]
    biasT = np.empty((8, 128, 4, 128), np.float32)
    for j in range(8):
        for hd in range(2):
            for pc in range(2):
                sl = pc * 2 + hd
                biasT[j, :, sl, :] = gathered[:, sl, :, j + 8 * hd]
    biasT = biasT.reshape(8, 128, 512)
    corr_std = np.ones((128, 64), np.float32)

    lw = [_layer_weights(inp, l) for l in layers]
    wA = np.stack([w[0] for w in lw])
    wP = np.stack([w[1] for w in lw])
    wG = np.stack([w[2] for w in lw])
    wO = np.stack([w[3] for w in lw])
    wE = np.stack([w[4] for w in lw])
    cols = np.concatenate([w[5] for w in lw], axis=1)
    rows = np.stack([w[6] for w in lw])
    p_all = np.asarray(inp["p"], np.float32)

    in_maps = []
    for c in range(NCORES):
        b, ch = c // 4, c % 4
        t0 = ch * CHUNK
        xs = np.zeros((T, D), np.float32)
        ps = np.zeros((nl, T, 256), np.float32)
        lo = t0 - HALO
        src0 = max(lo, 0)
        xs[src0 - lo:, :] = x_full[b, src0:t0 + CHUNK, :]
        for li, l in enumerate(layers):
            ps[li, src0 - lo:, :] = p_all[l, b, src0:t0 + CHUNK, :]
        xT = np.ascontiguousarray(xs.T).reshape(DC, 128, T)
        pT = np.ascontiguousarray(ps.transpose(0, 2, 1)).reshape(nl, 2, 128, T).transpose(0, 2, 1, 3)
        first = ch == 0
        in_maps.append({
            "xT": xT, "pT": np.ascontiguousarray(pT),
            "wA": wA, "wP": wP, "wG": wG, "wO": wO, "wE": wE,
            "cols": np.ascontiguousarray(cols), "rows": rows,
            "biasT": biasT, "mask01": mask01,
            "corr": corr_first if first else corr_std,
            "flag": np.full((128, 8), 0.0 if first else 1.0, np.float32),
            "ident": np.eye(128, dtype=np.float32),
        })
    res = run_bass_kernel_spmd(nc, in_maps, core_ids=list(range(NCORES)))
    out = np.empty((B, S, D), np.float32)
    for c in range(NCORES):
        b, ch = c // 4, c % 4
        oT = np.asarray(res.results[c]["outT"]).reshape(D, CHUNK)
        out[b, ch * CHUNK:(ch + 1) * CHUNK, :] = oT.T
    return out


FUSED = True


def kernel(**inputs):
    x = np.asarray(inputs["x"], np.float32)
    if FUSED:
        return _run_layers(x, inputs, list(range(DEPTH)))
    for l in range(DEPTH):
        x = _run_layers(x, inputs, [l])
    return x
```

```python
import math
from contextlib import ExitStack

import numpy as np
import concourse.bass as bass
import concourse.mybir as mybir
from concourse.bass_utils import run_bass_kernel_spmd

F32 = mybir.dt.float32
BF16 = mybir.dt.bfloat16
U8 = mybir.dt.uint8
I32 = mybir.dt.int32
AF = mybir.ActivationFunctionType
ALU = mybir.AluOpType
ESZ = {F32: 4, BF16: 2, U8: 1, I32: 4}

D = 2048
DC = 16
DEPTH = 4
B, S = 2, 4096
NCORES = 8
CHUNK = 1024
HALO = 512
T = CHUNK + HALO
NT = T // 128
G = 512
GT = 4
NG = T // G
ALPHA = (2.0 * DEPTH) ** 0.25
LN_EPS = 1e-5
NCOL = 82
NA = 35
CH_KD = 0
CH_V = 2
CH_Q = 3
CH_GA = 11
CH_U = 19
CH_GP = 27
POOL_WINDOWS = (2, 4, 8, 16)

NSLOT = 3
WLOOK = 2
USE_WLOOK = False
SYNC_SAME_ENGINE_ALL = True
NF = 8


class _Op:
    __slots__ = ("eng", "fn", "is_dma", "key", "idx", "cnt", "signal", "waits", "semval")

    def __init__(self, eng, fn, is_dma=False, key=None):
        self.eng = eng
        self.fn = fn
        self.is_dma = is_dma
        self.key = key
        self.idx = -1
        self.cnt = 0
        self.signal = False
        self.waits = []
        self.semval = 0


class Sched:
    ENGS = ("pe", "act", "dve", "pool", "sp")

    def __init__(self):
        self.ops = {e: [] for e in self.ENGS}
        self.recs = {}
        self.waited = {e: {} for e in self.ENGS}
        self.dma_cnt = {}
        self.dma_keys = []

    @staticmethod
    def interval(ap):
        name = ap.tensor.name
        if name.startswith("arena"):
            space = "sb"
        elif name.startswith("psum"):
            space = "ps"
        else:
            return None
        dims = ap.ap
        pstep = dims[0][0]
        off = ap.offset % pstep
        ext = 1
        for st, cn in dims[1:]:
            ext += (cn - 1) * st
        esz = ESZ[ap.dtype]
        if space == "ps":
            b0 = (off * esz) // 2048
            b1 = ((off + ext) * esz - 1) // 2048
            return (space, b0 * 2048, (b1 + 1) * 2048)
        return (space, off * esz, (off + ext) * esz)

    def _need(self, op, prod, raw):
        if prod is op:
            return
        if prod.is_dma:
            k = ("dma", prod.key)
            if self.waited[op.eng].get(k, 0) >= prod.cnt:
                return
            self.waited[op.eng][k] = prod.cnt
            op.waits.append(prod)
            prod.signal = True
            return
        if prod.eng == op.eng and not op.is_dma:
            if op.eng == "pe":
                return
            if not raw and not SYNC_SAME_ENGINE_ALL:
                return
        k = prod.eng
        if self.waited[op.eng].get(k, -1) >= prod.idx:
            return
        self.waited[op.eng][k] = prod.idx
        op.waits.append(prod)
        prod.signal = True

    def _access(self, op, ap, is_write):
        iv = self.interval(ap)
        if iv is None:
            return
        space, s, e = iv
        recs = self.recs.setdefault(space, [])
        keep = []
        for r in recs:
            rs, re_, rop, rw = r
            if rs < e and s < re_:
                if is_write:
                    self._need(op, rop, raw=False)
                    if s <= rs and re_ <= e:
                        continue
                elif rw:
                    self._need(op, rop, raw=True)
                elif rop.eng == op.eng and rs == s and re_ == e and not rop.is_dma and not op.is_dma:
                    continue
            keep.append(r)
        keep.append((s, e, op, is_write))
        self.recs[space] = keep

    def add(self, eng, fn, reads=(), writes=(), dma_key=None):
        op = _Op(eng, fn, is_dma=dma_key is not None, key=dma_key)
        lst = self.ops[eng]
        op.idx = len(lst)
        if op.is_dma:
            if dma_key not in self.dma_cnt:
                self.dma_cnt[dma_key] = 0
                self.dma_keys.append(dma_key)
            self.dma_cnt[dma_key] += 1
            op.cnt = self.dma_cnt[dma_key]
        for ap in reads:
            self._access(op, ap, False)
        for ap in writes:
            self._access(op, ap, True)
        lst.append(op)
        return op

    def emit(self, nc, stack, final_dma_keys=()):
        sems = {e: stack.enter_context(nc.semaphore("s_" + e)) for e in self.ENGS}
        dsems = {k: stack.enter_context(nc.semaphore("d_%d" % i)) for i, k in enumerate(self.dma_keys)}
        for e in self.ENGS:
            c = 0
            for op in self.ops[e]:
                if op.signal and not op.is_dma:
                    c += 1
                    op.semval = c
        block = stack.enter_context(nc.Block())
        handles = {"pe": block.tensor, "act": block.scalar, "dve": block.vector,
                   "pool": block.gpsimd, "sp": block.sync}

        def make(ename):
            def body(eng):
                for op in self.ops[ename]:
                    for p in op.waits:
                        if p.is_dma:
                            eng.wait_ge(dsems[p.key], 16 * p.cnt)
                        else:
                            eng.wait_ge(sems[p.eng], p.semval)
                    inst = op.fn(eng)
                    if op.is_dma:
                        inst.then_inc(dsems[op.key], 16)
                    elif op.signal:
                        inst.then_inc(sems[ename], 1)
                if ename == "sp":
                    for k in final_dma_keys:
                        if k in dsems:
                            eng.wait_ge(dsems[k], 16 * self.dma_cnt[k])
            return body

        for e in self.ENGS:
            handles[e](make(e))


class Arena:
    def __init__(self, tensor, size):
        self.t = tensor
        self.size = size
        self.top = 0

    def alloc(self, nbytes):
        off = (self.top + 31) // 32 * 32
        self.top = off + nbytes
        assert self.top <= self.size, ("SBUF arena overflow", self.top, self.size)
        return off

    def view(self, off, shape, dtype):
        n = int(np.prod(shape)) * ESZ[dtype]
        v = self.t[:, off:off + n].bitcast(dtype)
        if len(shape) == 1:
            return v
        names = "abcde"[:len(shape)]
        pat = "p (" + " ".join(names) + ") -> p " + " ".join(names)
        kw = {names[i]: shape[i] for i in range(len(shape) - 1)}
        return v.rearrange(pat, **kw)

    def new(self, shape, dtype):
        n = int(np.prod(shape)) * ESZ[dtype]
        off = self.alloc(n)
        return self.view(off, shape, dtype), off


def build_program(n_layers, ple_dtype=F32, wlist=None, record=False):
    L = n_layers
    nc = bass.Bass("TRN2", target_bir_lowering=False)
    dr = {}

    def dram(name, shape, kind="ExternalInput"):
        dr[name] = nc.dram_tensor(name, list(shape), F32, kind=kind).ap()
        return dr[name]

    xT_d = dram("xT", [DC, 128, T])
    pT_d = dram("pT", [L, 128, 2, T])
    wA_d = dram("wA", [L, NA, 128, 2048])
    wP_d = dram("wP", [L, 4, 128, 512])
    wG_d = dram("wG", [L, DC, 128, 2048])
    wO_d = dram("wO", [L, DC, 128, 2048])
    wE_d = dram("wE", [L, DC, 128, 256])
    cols_d = dram("cols", [128, L * NCOL])
    rows_d = dram("rows", [L, 1152])
    biasT_d = dram("biasT", [8, 128, 512])
    mask_d = dram("mask01", [128, 512])
    corr_d = dram("corr", [128, 64])
    flag_d = dram("flag", [128, 8])
    ident_d = dram("ident", [128, 128])
    out_d = dram("outT", [DC, 128, CHUNK], kind="ExternalOutput")

    stack = ExitStack()
    ARENA_BYTES = 211456
    arena_t = stack.enter_context(nc.sbuf_tensor("arena", [128, ARENA_BYTES], U8))
    psum_t = stack.enter_context(nc.psum_tensor("psum", [128, 8, 512], F32))
    A = Arena(arena_t, ARENA_BYTES)
    S_ = Sched()

    def PS(b):
        return psum_t[:, b, :]

    X2, x2_off = A.new([DC, NG, 2, G], BF16)
    WS, _ = A.new([NSLOT, 2048], BF16)
    WPL, _ = A.new([2, 512], BF16)
    WPE, _ = A.new([2, 256], BF16)
    KD, _ = A.new([2, 5, 128], BF16)
    VP, _ = A.new([5, 2, 192], BF16)
    UCY, _ = A.new([8, 16], F32)
    CORR, _ = A.new([4, 16], F32)
    ET, _ = A.new([8, 512], BF16)
    COLS, _ = A.new([L * NCOL], F32)
    ESK, _ = A.new([L * 8], F32)
    PSH, _ = A.new([L * 8], F32)
    ROWB, _ = A.new([1152], BF16)
    R1, _ = A.new([128], BF16)
    ONES, _ = A.new([128], BF16)
    IDENT, _ = A.new([128], BF16)
    OPAD, _ = A.new([192], BF16)
    OPADF, _ = A.new([192], BF16)
    FLAG, _ = A.new([8], F32)
    AB, _ = A.new([DC, G], BF16)
    PTB, _ = A.new([1, 2, G], BF16)
    FT, _ = A.new([NF, G], F32)
    ubase = A.alloc(0)
    UB, _ = A.new([4, 528], F32)
    QAB, _ = A.new([2, GT, 2, 128], BF16)
    PT2, _ = A.new([2, G], BF16)
    DF, _ = A.new([2, 2, G], BF16)
    top_ab = A.top
    A.top = ubase
    PLE, _ = A.new([DC, G], ple_dtype)
    YH, _ = A.new([2, G], BF16)
    YS, _ = A.new([2, G], BF16)
    A.top = max(A.top, top_ab)

    def XH(dc, g):
        return X2[:, dc, g, 0, :]

    def XL(dc, g):
        return X2[:, dc, g, 1, :]

    def YV(dc, g):
        off = x2_off + ((dc * NG + g) * 2) * G * 2
        return A.view(off, [G], F32)

    def F(k):
        return FT[:, k, :]

    def Fi(k):
        return FT[:, k, :].bitcast(I32)

    def dma(eng, out, in_, key):
        S_.add(eng, lambda e: e.dma_start(out=out, in_=in_), reads=[in_], writes=[out], dma_key=key)

    def mm(out, pairs):
        n = len(pairs)
        rd = []
        for a, b in pairs:
            rd.append(a)
            rd.append(b)

        def fn(e):
            inst = None
            for i, (lhsT, rhs) in enumerate(pairs):
                inst = e.matmul(out, lhsT=lhsT, rhs=rhs, start=(i == 0), stop=(i == n - 1))
            return inst
        S_.add("pe", fn, reads=rd, writes=[out])

    def act(out, in_, func, bias=None, scale=1.0, eng="act"):
        rd = [in_]
        kw = {}
        if bias is not None:
            kw["bias"] = bias
            if not isinstance(bias, (int, float)):
                rd.append(bias)
        if not isinstance(scale, (int, float)):
            rd.append(scale)
        S_.add(eng, lambda e: e.activation(out=out, in_=in_, func=func, scale=scale, **kw),
               reads=rd, writes=[out])

    def tt(out, in0, in1, op, eng="dve"):
        S_.add(eng, lambda e: e.tensor_tensor(out=out, in0=in0, in1=in1, op=op), reads=[in0, in1], writes=[out])

    def ts(out, in0, s1, op0, s2=None, op1=None, eng="dve"):
        rd = [in0]
        if not isinstance(s1, (int, float)):
            rd.append(s1)
        if s2 is not None and not isinstance(s2, (int, float)):
            rd.append(s2)

        def fn(e):
            if op1 is None:
                return e.tensor_scalar(out=out, in0=in0, scalar1=s1, scalar2=None, op0=op0)
            return e.tensor_scalar(out=out, in0=in0, scalar1=s1, scalar2=s2, op0=op0, op1=op1)
        S_.add(eng, fn, reads=rd, writes=[out])

    def stt(out, in0, scalar, in1, op0, op1, eng="dve"):
        rd = [in0, in1]
        if not isinstance(scalar, (int, float)):
            rd.append(scalar)
        S_.add(eng, lambda e: e.scalar_tensor_tensor(out=out, in0=in0, scalar=scalar, in1=in1, op0=op0, op1=op1),
               reads=rd, writes=[out])

    def cp(out, in_, eng="dve"):
        if eng == "act":
            S_.add(eng, lambda e: e.activation(out=out, in_=in_, func=AF.Copy), reads=[in_], writes=[out])
        else:
            S_.add(eng, lambda e: e.tensor_copy(out=out, in_=in_), reads=[in_], writes=[out])

    def recip(out, in_):
        S_.add("dve", lambda e: e.reciprocal(out=out, in_=in_), reads=[in_], writes=[out])

    def memset(ap, val, eng="dve"):
        S_.add(eng, lambda e: e.memset(ap, val), writes=[ap])

    wstate = {"n": 0, "emitted": 0}
    wrec = []

    def wload(src):
        n = wstate["n"]
        wstate["n"] += 1
        s = n % NSLOT
        if wlist is None:
            wrec.append(src)
            dma("pool", WS[:, s, :], dr[src[0]][src[1], src[2]], ("w", s))
        else:
            upto = min(n + 1 + WLOOK, len(wlist))
            while wstate["emitted"] < upto:
                m = wstate["emitted"]
                wsrc = wlist[m]
                dma("pool", WS[:, m % NSLOT, :], dr[wsrc[0]][wsrc[1], wsrc[2]], ("w", m % NSLOT))
                wstate["emitted"] += 1
        return WS[:, s, :].rearrange("p (a b) -> p a b", a=16)

    small = {"wp": 0, "we": 0}

    dma("sp", COLS, cols_d, "c_cols")
    dma("sp", FLAG, flag_d, "c_flag")
    memset(ONES, 1.0)
    memset(OPAD, 0.0)
    memset(OPAD[:, 64:128], 1.0)
    memset(R1, 0.0)
    memset(R1[0:1, :], 1.0)
    memset(ROWB, 0.0)
    memset(VP.rearrange("p a b c -> p (a b c)"), 0.0)
    ts(OPADF, OPAD, FLAG[:, 0:1], ALU.mult)
    for l in range(L):
        c0 = l * NCOL
        act(ESK[:, l * 8:(l + 1) * 8], COLS[:, c0 + 34:c0 + 42], AF.Exp)
        ts(PSH[:, l * 8:(l + 1) * 8], COLS[:, c0 + 26:c0 + 34], 0.5, ALU.mult)
    def xload(g_, dc_, nbuf=2, base=6):
        k_ = base + (dc_ % nbuf)
        fa = F(k_)
        dma("sp", fa, xT_d[dc_][:, g_ * G:(g_ + 1) * G], ("c_x", k_))
        act(XH(dc_, g_), fa, AF.Copy)
        tt(XL(dc_, g_), fa, XH(dc_, g_), ALU.subtract)

    for dc in range(DC):
        xload(0, dc, nbuf=8, base=0)
    dma("sp", CORR.rearrange("p a b -> p (a b)"), corr_d, "c_corr")
    dma("sp", F(3), mask_d, "c_mask")
    ts(F(3), F(3), 240000.0, ALU.mult, -240000.0, ALU.add)
    for j in range(8):
        fa = F(4 + (j % 2))
        dma("sp", fa, biasT_d[j], ("c_bias", j % 2))
        stt(ET[:, j, :], fa, 8.0, F(3), ALU.mult, ALU.add)
    dma("sp", F(2)[:, 0:128], ident_d, "c_ident")
    cp(IDENT, F(2)[:, 0:128])

    deferred = []

    def drain(n):
        while deferred and n > 0:
            deferred.pop(0)()
            n -= 1

    for g_ in range(1, NG):
        for dc_ in range(DC):
            deferred.append(lambda g_=g_, dc_=dc_: xload(g_, dc_))

    def layer_group(l, g, last_layer):
        c0 = l * NCOL
        i_full = l + 5 - L
        if g == 0:
            i_f = min(max(i_full, 0), GT)
            i_k = max(i_f - 1, 0)
        else:
            i_f = i_k = 0
        carry_only = i_f >= GT
        cf, ck = 128 * i_f, 128 * i_k

        def col(k):
            return COLS[:, c0 + k:c0 + k + 1]

        def c(ap):
            return ap[:, cf:]

        Xg = [XH(dc, g) for dc in range(DC)]

        def hchunk(cid, bank, cc=None):
            cc = cf if cc is None else cc
            W = wload(("wA", l, cid))
            mm(PS(bank)[:, cc:], [(W[:, dc, :], Xg[dc][:, cc:]) for dc in range(DC)])

        hchunk(CH_KD, 0, ck)
        act(KD[:, 0, 1 + i_k:5, :].rearrange("p a b -> p (a b)"), PS(0)[:, ck:], AF.Identity, bias=col(8))
        W = wload(("wA", l, CH_V))
        for i in range(i_k, GT):
            pairs = [(Xg[dc][:, i * 128:(i + 1) * 128], W[:, dc, :]) for dc in range(DC)]
            pairs.append((R1, ROWB[:, 0:128]))
            mm(PS(2)[:, i * 128:(i + 1) * 128], pairs)
        for i in range(i_k, GT):
            cp(VP[:, 1 + i, :, 64:128], PS(2)[:, i * 128:(i + 1) * 128].rearrange("p (a b) -> p a b", a=2), eng="act")

        if not carry_only:
            for b_ in range(2):
                memset(QAB[64:128, b_, :, 0, :], 0.0)
                memset(QAB[0:64, b_, :, 1, :], 0.0)
            hb = {"n": 0}
            blocks = list(range(i_f, GT))
            nb = len(blocks)
            seq = []
            for t_ in range(nb + 2):
                if 2 <= t_ and t_ - 2 < nb:
                    seq.append(("D", blocks[t_ - 2]))
                if t_ < nb:
                    seq.append(("S", blocks[t_]))
            done_d = [x[1] for x in seq if x[0] == "D"]
            for b_ in blocks:
                if b_ not in done_d:
                    seq.append(("D", b_))
            k1 = 0
            while k1 < len(seq) and seq[k1][0] == "S":
                k1 += 1
            rest = seq[k1:]
            st1, st3, st5 = seq[:k1], rest[:len(rest) // 2], rest[len(rest) // 2:]

            def next_hbank():
                b_ = (0, 1, 4)[hb["n"] % 3]
                hb["n"] += 1
                return b_

            def attn_chunk(j, qa, qb, gbank):
                den, num = PS(5), PS(6 + (j % 2))
                sb = [2, 3]

                def s_block(i):
                    sp_ = PS(sb[i % 2])
                    skip_prev = (g == 0 and i == 0)
                    items = [(sp_, IDENT, ET[:, j, :])]
                    qsrc = qa[:, i, :, :].rearrange("p a b -> p (a b)")
                    for pc in range(2):
                        if skip_prev and pc == 0:
                            continue
                        items.append((sp_[:, pc * 256:(pc + 1) * 256], KD[:, 0, i + pc, :], qsrc))
                    rd = []
                    for o_, a_, b_ in items:
                        rd += [a_, b_]
                    n_ = len(items)

                    def fn(e, items=items, n_=n_):
                        inst = None
                        for ii, (o_, a_, b_) in enumerate(items):
                            inst = e.matmul(o_, lhsT=a_, rhs=b_, start=(ii == 0), stop=(ii == n_ - 1))
                        return inst
                    S_.add("pe", fn, reads=rd, writes=[sp_])
                    act(PT2[:, i % 2, :], sp_, AF.Exp, scale=0.125)

                def dn_block(i):
                    p2 = PT2[:, i % 2, :]
                    dpairs, npairs = [], []
                    for hd in range(2):
                        for pc in range(2):
                            if g == 0 and i == 0 and pc == 0:
                                continue
                            rhs = p2[:, (pc * 2 + hd) * 128:(pc * 2 + hd + 1) * 128]
                            op_ = OPADF if (g == 1 and i == 0 and pc == 0) else OPAD
                            lo_ = 64 if hd == 0 else 0
                            dpairs.append((op_[:, lo_:lo_ + 128], rhs))
                            npairs.append((VP[:, i + pc, hd, lo_:lo_ + 128], rhs))
                    mm(den[:, i * 128:(i + 1) * 128], dpairs)
                    mm(num[:, i * 128:(i + 1) * 128], npairs)

                def gate_pre():
                    k2 = 2 * (j % 2)
                    f0, f1 = c(F(k2)), c(F(k2 + 1))
                    act(f0, c(PS(gbank)), AF.Identity, bias=col(10 + j))
                    act(f1, f0, AF.Tanh, scale=0.5)

                def gate_chain():
                    k2 = 2 * (j % 2)
                    f0, f1 = c(F(k2)), c(F(k2 + 1))
                    stt(f1, f1, 1.0, f0, ALU.add, ALU.mult)
                    act(f0, c(den), AF.Identity, bias=ESK[:, l * 8 + j:l * 8 + j + 1])
                    recip(f0, f0)
                    tt(f0, f0, f1, ALU.mult)
                    stt(c(AB[:, j, :]), c(num), 0.5, f0, ALU.mult, ALU.mult)

                def run(stage):
                    for kind, i in stage:
                        (s_block if kind == "S" else dn_block)(i)

                return run, gate_chain, gate_pre

            pend = None
            for j in range(8):
                qbank = next_hbank()
                hchunk(CH_Q + j, qbank)
                qa = QAB[:, j % 2, :, :, :]
                qb = qa
                act(QAB[0:64, j % 2, i_f:, 0, :], PS(qbank)[0:64, cf:].rearrange("p (a b) -> p a b", b=128),
                    AF.Identity, bias=COLS[0:64, c0 + j:c0 + j + 1])
                act(QAB[64:128, j % 2, i_f:, 1, :], PS(qbank)[64:128, cf:].rearrange("p (a b) -> p a b", b=128),
                    AF.Identity, bias=COLS[64:128, c0 + j:c0 + j + 1])
                if pend is not None:
                    pend[0](st3)
                gbank = next_hbank()
                hchunk(CH_GA + j, gbank)
                if pend is not None:
                    pend[0](st5)
                nxt = attn_chunk(j, qa, qb, gbank)
                nxt[2]()
                nxt[0](st1)
                if pend is not None:
                    pend[1]()
                pend = nxt
            pend[0](st3)
            pend[0](st5)
            pend[1]()

        ubs = {"n": 0}

        def pool_u(pg):
            w = POOL_WINDOWS[pg]
            nsteps = {2: 1, 4: 2, 8: 3, 16: 4}[w]
            for cc in range(2):
                cu = 2 * pg + cc
                bank = cu % 2
                hchunk(CH_U + cu, bank, ck)
                ua = UB[:, ubs["n"] % 2, :]
                ubs["n"] += 1
                tb = [UB[:, 2, :], UB[:, 3, :]]
                act(ua[:, 16 + ck:], PS(bank)[:, ck:], AF.Identity, bias=col(74 + cu))
                if g >= 1:
                    cp(ua[:, 0:16], UCY[:, cu, :], eng="dve")
                else:
                    memset(ua[:, 0:16], 0.0)
                if g < NG - 1:
                    if g == 0:
                        ts(UCY[:, cu, :], ua[:, 512:528], FLAG[:, 0:1], ALU.mult)
                    else:
                        cp(UCY[:, cu, :], ua[:, 512:528], eng="dve")
                if carry_only:
                    continue
                lo = 0 if g >= 1 else 16 + ck
                src = ua
                sh = 1
                for st_ in range(nsteps):
                    dst = tb[st_ % 2]
                    lo2 = lo + sh
                    tt(dst[:, lo2:], src[:, lo2:], src[:, lo2 - sh:528 - sh], ALU.add)
                    src, lo, sh = dst, lo2, sh * 2
                if g == 1:
                    tt(src[:, 16:32], src[:, 16:32], CORR[:, pg, :], ALU.mult)
                stt(c(DF[:, pg % 2, cc, :]), src[:, 16 + cf:], 1.0 / w, ua[:, 16 + cf:], ALU.mult, ALU.subtract)

        def pool_mix(pg):
            s_ = small["wp"] % 2
            small["wp"] += 1
            dma("pool", WPL[:, s_, :], wP_d[l, pg], ("wp", s_))
            Wp = WPL[:, s_, :].rearrange("p (a b) -> p a b", a=2)
            gbanks = []
            for dd in range(2):
                jb = 2 * pg + dd
                hchunk(CH_GP + jb, 6 + dd)
            for dd in range(2):
                mbank = 4 + dd
                mm(c(PS(mbank)), [(Wp[:, kc, dd * 128:(dd + 1) * 128], c(DF[:, pg % 2, kc, :])) for kc in range(2)])
                jb = 2 * pg + dd
                gbank = 6 + dd
                k2 = 2 * (jb % 2)
                f0, f1 = c(F(k2)), c(F(k2 + 1))
                act(f0, c(PS(gbank)), AF.Identity, bias=col(18 + jb))
                act(f1, f0, AF.Tanh, scale=0.5)
                stt(f1, f1, 1.0, f0, ALU.add, ALU.mult)
                stt(c(AB[:, 8 + jb, :]), c(PS(mbank)), PSH[:, l * 8 + jb:l * 8 + jb + 1], f1, ALU.mult, ALU.mult)
                drain(1)

        for pg in range(5):
            if pg < 4:
                pool_u(pg)
            if pg > 0 and not carry_only:
                pool_mix(pg - 1)

        if g < NG - 1:
            cp(KD[:, 0, 0, :], KD[:, 0, 4, :], eng="dve")
            if g == 0:
                ts(VP[:, 0, :, :], VP[:, 4, :, :], FLAG[:, 0:1], ALU.mult)
            else:
                cp(VP[:, 0, :, :], VP[:, 4, :, :], eng="dve")
        if carry_only:
            return

        pb = 0
        dma("pool", PTB[:, pb, :, :], pT_d[l][:, :, g * G:(g + 1) * G], ("pt", pb))
        for dch in range(DC):
            W = wload(("wG", l, dch))
            gb = dch % 2
            mm(c(PS(gb)), [(W[:, dc, :], c(Xg[dc])) for dc in range(DC)])
            s_ = small["we"] % 2
            small["we"] += 1
            dma("pool", WPE[:, s_, :], wE_d[l, dch], ("we", s_))
            We = WPE[:, s_, :].rearrange("p (a b) -> p a b", a=2)
            eb = 2 + dch % 2
            mm(c(PS(eb)), [(We[:, pc, :], c(PTB[:, pb, pc, :])) for pc in range(2)])
            f0 = c(F(dch % 2))
            act(f0, c(PS(gb)), AF.Tanh, scale=0.5)
            stt(c(PLE[:, dch, :]), f0, 1.0, c(PS(eb)), ALU.add, ALU.mult)
            drain(1)

        stats_pend = []

        def stats_mm(dch, yh, ys):
            S_.add("pe", (lambda e: e.matmul(c(PS(6)), lhsT=ONES, rhs=yh, start=(dch == 0), stop=(dch == DC - 1))),
                   reads=[ONES, yh], writes=[c(PS(6))])
            S_.add("pe", (lambda e: e.matmul(c(PS(7)), lhsT=ONES, rhs=ys, start=(dch == 0), stop=(dch == DC - 1))),
                   reads=[ONES, ys], writes=[c(PS(7))])

        for dch in range(DC):
            W = wload(("wO", l, dch))
            mb = 4 + dch % 2
            mm(c(PS(mb)), [(W[:, cc, :], c(AB[:, cc, :])) for cc in range(DC)])
            if len(stats_pend) >= 1:
                stats_mm(*stats_pend.pop(0))
            f0 = c(F(2 + dch % 2))
            tt(f0, c(XH(dch, g)), c(XL(dch, g)), ALU.add)
            stt(f0, f0, ALPHA, c(PS(mb)), ALU.mult, ALU.add)
            y = c(YV(dch, g))
            stt(y, c(PLE[:, dch, :]), 0.5, f0, ALU.mult, ALU.add)
            yh, ys = c(YH[:, dch % 2, :]), c(YS[:, dch % 2, :])
            act(yh, y, AF.Copy)
            act(ys, y, AF.Square)
            stats_pend.append((dch, yh, ys))
        while stats_pend:
            stats_mm(*stats_pend.pop(0))

        drain(10 ** 6)
        mean, var, r, tmp = c(F(7)), c(F(4)), c(F(6)), c(F(5))
        ts(mean, c(PS(6)), 1.0 / D, ALU.mult)
        tt(tmp, mean, mean, ALU.mult)
        stt(var, c(PS(7)), 1.0 / D, tmp, ALU.mult, ALU.subtract)
        ts(var, var, LN_EPS, ALU.add)
        ts(r, var, 0.25, ALU.mult, 1.0, ALU.add)
        recip(r, r)
        for _ in range(5):
            tt(tmp, r, r, ALU.mult)
            tt(tmp, tmp, var, ALU.mult)
            ts(tmp, tmp, -0.5, ALU.mult, 1.5, ALU.add)
            tt(r, r, tmp, ALU.mult)
        tt(mean, mean, r, ALU.mult)
        def ln_a(dch):
            y = c(YV(dch, g))
            tt(y, y, r, ALU.mult)
            tt(y, y, mean, ALU.subtract)

        def ln_b(dch):
            y = c(YV(dch, g))
            f1 = c(F(4 + dch % 2))
            act(f1, y, AF.Identity, bias=col(58 + dch), scale=col(42 + dch))
            if last_layer:
                dma("sp", out_d[dch][:, (g - 1) * G + cf:g * G], f1, ("out", dch % 2))
            else:
                act(c(XH(dch, g)), f1, AF.Copy)
                tt(c(XL(dch, g)), f1, c(XH(dch, g)), ALU.subtract)

        for k in range(DC + 2):
            def item(k=k):
                if k < DC:
                    ln_a(k)
                if k >= 2:
                    ln_b(k - 2)
            deferred.append(item)

    for l in range(L):
        dma("pool", ROWB[0:1, :], rows_d[l:l + 1, :], "rows")
        for g in range(NG):
            layer_group(l, g, l == L - 1)

    drain(10 ** 6)
    if record:
        stack.close()
        return wrec
    S_.emit(nc, stack, final_dma_keys=[("out", 0), ("out", 1)])
    print("arena top", A.top, "of", ARENA_BYTES, "n_ops", {e: len(v) for e, v in S_.ops.items()})
    stack.close()
    return nc


def _t5_bucket(dist):
    max_exact = 16
    d = np.maximum(dist, 0)
    d_f = np.maximum(d, 1).astype(np.float64)
    large = max_exact + (np.log(d_f / max_exact) / math.log(128 / max_exact) * (32 - max_exact)).astype(np.int64)
    large = np.minimum(large, 31)
    return np.where(d < max_exact, d, large)


def _const_tables():
    k = np.arange(128)[:, None]
    q = np.arange(128)[None, :]
    dist = np.zeros((128, 4, 128), np.int64)
    mask = np.zeros((128, 4, 128), np.float32)
    for hd in range(2):
        for pc in range(2):
            d = q + 128 - (k + 128 * pc)
            sl = pc * 2 + hd
            dist[:, sl, :] = d
            mask[:, sl, :] = ((d >= 0) & (d < 128)).astype(np.float32)
    bucket = _t5_bucket(np.clip(dist, 0, 255))
    j = np.arange(128)[:, None]
    t = np.arange(128)[None, :]
    ptm = np.zeros((128, 8, 128), np.float32)
    ptf = np.zeros((128, 4, 128), np.float32)
    for pg, w in enumerate(POOL_WINDOWS):
        ptm[:, pg, :] = np.where(j - 128 >= t - w + 1, 1.0 / w, 0.0)
        cur = np.where((j <= t) & (j >= t - w + 1), 1.0 / w, 0.0) - (j == t)
        ptm[:, 4 + pg, :] = cur
        cnt = np.minimum(t + 1, w).astype(np.float64)
        ptf[:, pg, :] = (np.where((j <= t) & (j >= t - w + 1), 1.0 / cnt, 0.0) - (j == t)).astype(np.float32)
    corr = np.ones((4, 16), np.float64)
    for pg, w in enumerate(POOL_WINDOWS):
        tt_ = np.arange(16)
        corr[pg] = w / np.minimum(tt_ + 1, w)
    corr_first = np.broadcast_to(corr.reshape(1, 64), (128, 64)).astype(np.float32).copy()
    return bucket, mask.reshape(128, 512), corr_first


def _layer_weights(inp, l):
    w_in = np.asarray(inp["w_in"][l], np.float32)
    b_in = np.asarray(inp["b_in"][l], np.float32)

    def chunk(cols):
        w = w_in[:, cols]
        return w.reshape(DC, 128, 128).transpose(1, 0, 2).reshape(128, 2048)

    wA = np.empty((NA, 128, 2048), np.float32)
    wA[CH_KD] = chunk(np.arange(1024, 1152))
    wA[CH_KD + 1] = wA[CH_KD]

    def hpair(j):
        return np.concatenate([np.arange(64 * j, 64 * j + 64), np.arange(512 + 64 * j, 512 + 64 * j + 64)])
    wA[CH_V] = chunk(np.arange(1152, 1280))
    for j in range(8):
        wA[CH_Q + j] = chunk(hpair(j))
        wA[CH_GA + j] = chunk(1280 + hpair(j))
        wA[CH_U + j] = chunk(np.arange(2304 + 128 * j, 2304 + 128 * j + 128))
        wA[CH_GP + j] = chunk(np.arange(3328 + 128 * j, 3328 + 128 * j + 128))
    wp = np.asarray(inp["w_pool"][l], np.float32)
    wP = wp.reshape(4, 2, 128, 256).transpose(0, 2, 1, 3).reshape(4, 128, 512)

    def ochunks(w, kc):
        r = w.reshape(kc, 128, DC, 128).transpose(2, 1, 0, 3)
        return np.ascontiguousarray(r).reshape(DC, 128, kc * 128)

    wG = ochunks(np.asarray(inp["w_gate_ple"][l], np.float32), 16)
    rperm = np.concatenate([hpair(j) for j in range(8)] + [np.arange(1024, 2048)])
    wO = ochunks(np.asarray(inp["w_out"][l], np.float32)[rperm], 16)
    wE = ochunks(np.asarray(inp["w_ple"][l], np.float32), 2)
    cols = np.empty((128, NCOL), np.float32)
    p = np.arange(128)
    for j in range(8):
        cols[:, j] = b_in[hpair(j)]
        cols[:, 10 + j] = b_in[1280 + hpair(j)]
        cols[:, 18 + j] = b_in[3328 + 128 * j + p]
        cols[:, 26 + j] = np.asarray(inp["pool_scale"][l], np.float32)[128 * j + p]
        sk = np.asarray(inp["attn_sinks"][l], np.float32)
        cols[:, 34 + j] = np.where(p < 64, sk[j], sk[8 + j])
    cols[:, 8] = b_in[1024 + p]
    cols[:, 9] = b_in[1024 + p]
    for dch in range(DC):
        cols[:, 42 + dch] = np.asarray(inp["ln_gain"][l], np.float32)[128 * dch + p]
        cols[:, 58 + dch] = np.asarray(inp["ln_bias"][l], np.float32)[128 * dch + p]
    for j in range(8):
        cols[:, 74 + j] = b_in[2304 + 128 * j + p]
    rows = np.concatenate([b_in[1152:1280], b_in[2304:3328]]).astype(np.float32)
    return wA, wP, wG, wO, wE, cols, rows


_PROG_CACHE = {}


def _get_prog(n_layers):
    if n_layers not in _PROG_CACHE:
        if USE_WLOOK:
            wl = build_program(n_layers, record=True)
            _PROG_CACHE[n_layers] = build_program(n_layers, wlist=wl)
        else:
            _PROG_CACHE[n_layers] = build_program(n_layers)
    return _PROG_CACHE[n_layers]


def _run_layers(x_full, inp, layers):
    nl = len(layers)
    nc = _get_prog(nl)
    bucket, mask01, corr_first = _const_tables()
    rel_bias = np.asarray(inp["rel_bias"], np.float32)
    gathered = rel_bias[bucket]
    biasT = np.empty((8, 128, 4, 128), np.float32)
    for j in range(8):
        for hd in range(2):
            for pc in range(2):
                sl = pc * 2 + hd
                biasT[j, :, sl, :] = gathered[:, sl, :, j + 8 * hd]
    biasT = biasT.reshape(8, 128, 512)
    corr_std = np.ones((128, 64), np.float32)

    lw = [_layer_weights(inp, l) for l in layers]
    wA = np.stack([w[0] for w in lw])
    wP = np.stack([w[1] for w in lw])
    wG = np.stack([w[2] for w in lw])
    wO = np.stack([w[3] for w in lw])
    wE = np.stack([w[4] for w in lw])
    cols = np.concatenate([w[5] for w in lw], axis=1)
    rows = np.stack([w[6] for w in lw])
    p_all = np.asarray(inp["p"], np.float32)

    in_maps = []
    for c in range(NCORES):
        b, ch = c // 4, c % 4
        t0 = ch * CHUNK
        xs = np.zeros((T, D), np.float32)
        ps = np.zeros((nl, T, 256), np.float32)
        lo = t0 - HALO
        src0 = max(lo, 0)
        xs[src0 - lo:, :] = x_full[b, src0:t0 + CHUNK, :]
        for li, l in enumerate(layers):
            ps[li, src0 - lo:, :] = p_all[l, b, src0:t0 + CHUNK, :]
        xT = np.ascontiguousarray(xs.T).reshape(DC, 128, T)
        pT = np.ascontiguousarray(ps.transpose(0, 2, 1)).reshape(nl, 2, 128, T).transpose(0, 2, 1, 3)
        first = ch == 0
        in_maps.append({
            "xT": xT, "pT": np.ascontiguousarray(pT),
            "wA": wA, "wP": wP, "wG": wG, "wO": wO, "wE": wE,
            "cols": np.ascontiguousarray(cols), "rows": rows,
            "biasT": biasT, "mask01": mask01,
            "corr": corr_first if first else corr_std,
            "flag": np.full((128, 8), 0.0 if first else 1.0, np.float32),
            "ident": np.eye(128, dtype=np.float32),
        })
    res = run_bass_kernel_spmd(nc, in_maps, core_ids=list(range(NCORES)))
    out = np.empty((B, S, D), np.float32)
    for c in range(NCORES):
        b, ch = c // 4, c % 4
        oT = np.asarray(res.results[c]["outT"]).reshape(D, CHUNK)
        out[b, ch * CHUNK:(ch + 1) * CHUNK, :] = oT.T
    return out


FUSED = True


def kernel(**inputs):
    x = np.asarray(inputs["x"], np.float32)
    if FUSED:
        return _run_layers(x, inputs, list(range(DEPTH)))
    for l in range(DEPTH):
        x = _run_layers(x, inputs, [l])
    return x
```

```python
import math
from contextlib import ExitStack

import numpy as np
import concourse.bass as bass
import concourse.mybir as mybir
from concourse.bass_utils import run_bass_kernel_spmd

F32 = mybir.dt.float32
BF16 = mybir.dt.bfloat16
U8 = mybir.dt.uint8
I32 = mybir.dt.int32
AF = mybir.ActivationFunctionType
ALU = mybir.AluOpType
ESZ = {F32: 4, BF16: 2, U8: 1, I32: 4}

D = 2048
DC = 16
DEPTH = 4
B, S = 2, 4096
NCORES = 8
CHUNK = 1024
HALO = 512
T = CHUNK + HALO
NT = T // 128
G = 512
GT = 4
NG = T // G
ALPHA = (2.0 * DEPTH) ** 0.25
LN_EPS = 1e-5
NCOL = 82
NA = 35
CH_KD = 0
CH_V = 2
CH_Q = 3
CH_GA = 11
CH_U = 19
CH_GP = 27
POOL_WINDOWS = (2, 4, 8, 16)

NSLOT = 3
WLOOK = 2
USE_WLOOK = False
SYNC_SAME_ENGINE_ALL = True
NF = 8


class _Op:
    __slots__ = ("eng", "fn", "is_dma", "key", "idx", "cnt", "signal", "waits", "semval")

    def __init__(self, eng, fn, is_dma=False, key=None):
        self.eng = eng
        self.fn = fn
        self.is_dma = is_dma
        self.key = key
        self.idx = -1
        self.cnt = 0
        self.signal = False
        self.waits = []
        self.semval = 0


class Sched:
    ENGS = ("pe", "act", "dve", "pool", "sp")

    def __init__(self):
        self.ops = {e: [] for e in self.ENGS}
        self.recs = {}
        self.waited = {e: {} for e in self.ENGS}
        self.dma_cnt = {}
        self.dma_keys = []

    @staticmethod
    def interval(ap):
        name = ap.tensor.name
        if name.startswith("arena"):
            space = "sb"
        elif name.startswith("psum"):
            space = "ps"
        else:
            return None
        dims = ap.ap
        pstep = dims[0][0]
        off = ap.offset % pstep
        ext = 1
        for st, cn in dims[1:]:
            ext += (cn - 1) * st
        esz = ESZ[ap.dtype]
        if space == "ps":
            b0 = (off * esz) // 2048
            b1 = ((off + ext) * esz - 1) // 2048
            return (space, b0 * 2048, (b1 + 1) * 2048)
        return (space, off * esz, (off + ext) * esz)

    def _need(self, op, prod, raw):
        if prod is op:
            return
        if prod.is_dma:
            k = ("dma", prod.key)
            if self.waited[op.eng].get(k, 0) >= prod.cnt:
                return
            self.waited[op.eng][k] = prod.cnt
            op.waits.append(prod)
            prod.signal = True
            return
        if prod.eng == op.eng and not op.is_dma:
            if op.eng == "pe":
                return
            if not raw and not SYNC_SAME_ENGINE_ALL:
                return
        k = prod.eng
        if self.waited[op.eng].get(k, -1) >= prod.idx:
            return
        self.waited[op.eng][k] = prod.idx
        op.waits.append(prod)
        prod.signal = True

    def _access(self, op, ap, is_write):
        iv = self.interval(ap)
        if iv is None:
            return
        space, s, e = iv
        recs = self.recs.setdefault(space, [])
        keep = []
        for r in recs:
            rs, re_, rop, rw = r
            if rs < e and s < re_:
                if is_write:
                    self._need(op, rop, raw=False)
                    if s <= rs and re_ <= e:
                        continue
                elif rw:
                    self._need(op, rop, raw=True)
                elif rop.eng == op.eng and rs == s and re_ == e and not rop.is_dma and not op.is_dma:
                    continue
            keep.append(r)
        keep.append((s, e, op, is_write))
        self.recs[space] = keep

    def add(self, eng, fn, reads=(), writes=(), dma_key=None):
        op = _Op(eng, fn, is_dma=dma_key is not None, key=dma_key)
        lst = self.ops[eng]
        op.idx = len(lst)
        if op.is_dma:
            if dma_key not in self.dma_cnt:
                self.dma_cnt[dma_key] = 0
                self.dma_keys.append(dma_key)
            self.dma_cnt[dma_key] += 1
            op.cnt = self.dma_cnt[dma_key]
        for ap in reads:
            self._access(op, ap, False)
        for ap in writes:
            self._access(op, ap, True)
        lst.append(op)
        return op

    def emit(self, nc, stack, final_dma_keys=()):
        sems = {e: stack.enter_context(nc.semaphore("s_" + e)) for e in self.ENGS}
        dsems = {k: stack.enter_context(nc.semaphore("d_%d" % i)) for i, k in enumerate(self.dma_keys)}
        for e in self.ENGS:
            c = 0
            for op in self.ops[e]:
                if op.signal and not op.is_dma:
                    c += 1
                    op.semval = c
        block = stack.enter_context(nc.Block())
        handles = {"pe": block.tensor, "act": block.scalar, "dve": block.vector,
                   "pool": block.gpsimd, "sp": block.sync}

        def make(ename):
            def body(eng):
                for op in self.ops[ename]:
                    for p in op.waits:
                        if p.is_dma:
                            eng.wait_ge(dsems[p.key], 16 * p.cnt)
                        else:
                            eng.wait_ge(sems[p.eng], p.semval)
                    inst = op.fn(eng)
                    if op.is_dma:
                        inst.then_inc(dsems[op.key], 16)
                    elif op.signal:
                        inst.then_inc(sems[ename], 1)
                if ename == "sp":
                    for k in final_dma_keys:
                        if k in dsems:
                            eng.wait_ge(dsems[k], 16 * self.dma_cnt[k])
            return body

        for e in self.ENGS:
            handles[e](make(e))


class Arena:
    def __init__(self, tensor, size):
        self.t = tensor
        self.size = size
        self.top = 0

    def alloc(self, nbytes):
        off = (self.top + 31) // 32 * 32
        self.top = off + nbytes
        assert self.top <= self.size, ("SBUF arena overflow", self.top, self.size)
        return off

    def view(self, off, shape, dtype):
        n = int(np.prod(shape)) * ESZ[dtype]
        v = self.t[:, off:off + n].bitcast(dtype)
        if len(shape) == 1:
            return v
        names = "abcde"[:len(shape)]
        pat = "p (" + " ".join(names) + ") -> p " + " ".join(names)
        kw = {names[i]: shape[i] for i in range(len(shape) - 1)}
        return v.rearrange(pat, **kw)

    def new(self, shape, dtype):
        n = int(np.prod(shape)) * ESZ[dtype]
        off = self.alloc(n)
        return self.view(off, shape, dtype), off


def build_program(n_layers, ple_dtype=F32, wlist=None, record=False):
    L = n_layers
    nc = bass.Bass("TRN2", target_bir_lowering=False)
    dr = {}

    def dram(name, shape, kind="ExternalInput"):
        dr[name] = nc.dram_tensor(name, list(shape), F32, kind=kind).ap()
        return dr[name]

    xT_d = dram("xT", [DC, 128, T])
    pT_d = dram("pT", [L, 128, 2, T])
    wA_d = dram("wA", [L, NA, 128, 2048])
    wP_d = dram("wP", [L, 4, 128, 512])
    wG_d = dram("wG", [L, DC, 128, 2048])
    wO_d = dram("wO", [L, DC, 128, 2048])
    wE_d = dram("wE", [L, DC, 128, 256])
    cols_d = dram("cols", [128, L * NCOL])
    rows_d = dram("rows", [L, 1152])
    biasT_d = dram("biasT", [8, 128, 512])
    mask_d = dram("mask01", [128, 512])
    corr_d = dram("corr", [128, 64])
    flag_d = dram("flag", [128, 8])
    ident_d = dram("ident", [128, 128])
    out_d = dram("outT", [DC, 128, CHUNK], kind="ExternalOutput")

    stack = ExitStack()
    ARENA_BYTES = 211456
    arena_t = stack.enter_context(nc.sbuf_tensor("arena", [128, ARENA_BYTES], U8))
    psum_t = stack.enter_context(nc.psum_tensor("psum", [128, 8, 512], F32))
    A = Arena(arena_t, ARENA_BYTES)
    S_ = Sched()

    def PS(b):
        return psum_t[:, b, :]

    X2, x2_off = A.new([DC, NG, 2, G], BF16)
    WS, _ = A.new([NSLOT, 2048], BF16)
    WPL, _ = A.new([2, 512], BF16)
    WPE, _ = A.new([2, 256], BF16)
    KD, _ = A.new([2, 5, 128], BF16)
    VP, _ = A.new([5, 2, 192], BF16)
    UCY, _ = A.new([8, 16], F32)
    CORR, _ = A.new([4, 16], F32)
    ET, _ = A.new([8, 512], BF16)
    COLS, _ = A.new([L * NCOL], F32)
    ESK, _ = A.new([L * 8], F32)
    PSH, _ = A.new([L * 8], F32)
    ROWB, _ = A.new([1152], BF16)
    R1, _ = A.new([128], BF16)
    ONES, _ = A.new([128], BF16)
    IDENT, _ = A.new([128], BF16)
    OPAD, _ = A.new([192], BF16)
    OPADF, _ = A.new([192], BF16)
    FLAG, _ = A.new([8], F32)
    AB, _ = A.new([DC, G], BF16)
    PTB, _ = A.new([1, 2, G], BF16)
    FT, _ = A.new([NF, G], F32)
    ubase = A.alloc(0)
    UB, _ = A.new([4, 528], F32)
    QAB, _ = A.new([2, GT, 2, 128], BF16)
    PT2, _ = A.new([2, G], BF16)
    DF, _ = A.new([2, 2, G], BF16)
    top_ab = A.top
    A.top = ubase
    PLE, _ = A.new([DC, G], ple_dtype)
    YH, _ = A.new([2, G], BF16)
    YS, _ = A.new([2, G], BF16)
    A.top = max(A.top, top_ab)

    def XH(dc, g):
        return X2[:, dc, g, 0, :]

    def XL(dc, g):
        return X2[:, dc, g, 1, :]

    def YV(dc, g):
        off = x2_off + ((dc * NG + g) * 2) * G * 2
        return A.view(off, [G], F32)

    def F(k):
        return FT[:, k, :]

    def Fi(k):
        return FT[:, k, :].bitcast(I32)

    def dma(eng, out, in_, key):
        S_.add(eng, lambda e: e.dma_start(out=out, in_=in_), reads=[in_], writes=[out], dma_key=key)

    def mm(out, pairs):
        n = len(pairs)
        rd = []
        for a, b in pairs:
            rd.append(a)
            rd.append(b)

        def fn(e):
            inst = None
            for i, (lhsT, rhs) in enumerate(pairs):
                inst = e.matmul(out, lhsT=lhsT, rhs=rhs, start=(i == 0), stop=(i == n - 1))
            return inst
        S_.add("pe", fn, reads=rd, writes=[out])

    def act(out, in_, func, bias=None, scale=1.0, eng="act"):
        rd = [in_]
        kw = {}
        if bias is not None:
            kw["bias"] = bias
            if not isinstance(bias, (int, float)):
                rd.append(bias)
        if not isinstance(scale, (int, float)):
            rd.append(scale)
        S_.add(eng, lambda e: e.activation(out=out, in_=in_, func=func, scale=scale, **kw),
               reads=rd, writes=[out])

    def tt(out, in0, in1, op, eng="dve"):
        S_.add(eng, lambda e: e.tensor_tensor(out=out, in0=in0, in1=in1, op=op), reads=[in0, in1], writes=[out])

    def ts(out, in0, s1, op0, s2=None, op1=None, eng="dve"):
        rd = [in0]
        if not isinstance(s1, (int, float)):
            rd.append(s1)
        if s2 is not None and not isinstance(s2, (int, float)):
            rd.append(s2)

        def fn(e):
            if op1 is None:
                return e.tensor_scalar(out=out, in0=in0, scalar1=s1, scalar2=None, op0=op0)
            return e.tensor_scalar(out=out, in0=in0, scalar1=s1, scalar2=s2, op0=op0, op1=op1)
        S_.add(eng, fn, reads=rd, writes=[out])

    def stt(out, in0, scalar, in1, op0, op1, eng="dve"):
        rd = [in0, in1]
        if not isinstance(scalar, (int, float)):
            rd.append(scalar)
        S_.add(eng, lambda e: e.scalar_tensor_tensor(out=out, in0=in0, scalar=scalar, in1=in1, op0=op0, op1=op1),
               reads=rd, writes=[out])

    def cp(out, in_, eng="dve"):
        if eng == "act":
            S_.add(eng, lambda e: e.activation(out=out, in_=in_, func=AF.Copy), reads=[in_], writes=[out])
        else:
            S_.add(eng, lambda e: e.tensor_copy(out=out, in_=in_), reads=[in_], writes=[out])

    def recip(out, in_):
        S_.add("dve", lambda e: e.reciprocal(out=out, in_=in_), reads=[in_], writes=[out])

    def memset(ap, val, eng="dve"):
        S_.add(eng, lambda e: e.memset(ap, val), writes=[ap])

    wstate = {"n": 0, "emitted": 0}
    wrec = []

    def wload(src):
        n = wstate["n"]
        wstate["n"] += 1
        s = n % NSLOT
        if wlist is None:
            wrec.append(src)
            dma("pool", WS[:, s, :], dr[src[0]][src[1], src[2]], ("w", s))
        else:
            upto = min(n + 1 + WLOOK, len(wlist))
            while wstate["emitted"] < upto:
                m = wstate["emitted"]
                wsrc = wlist[m]
                dma("pool", WS[:, m % NSLOT, :], dr[wsrc[0]][wsrc[1], wsrc[2]], ("w", m % NSLOT))
                wstate["emitted"] += 1
        return WS[:, s, :].rearrange("p (a b) -> p a b", a=16)

    small = {"wp": 0, "we": 0}

    dma("sp", COLS, cols_d, "c_cols")
    dma("sp", FLAG, flag_d, "c_flag")
    memset(ONES, 1.0)
    memset(OPAD, 0.0)
    memset(OPAD[:, 64:128], 1.0)
    memset(R1, 0.0)
    memset(R1[0:1, :], 1.0)
    memset(ROWB, 0.0)
    memset(VP.rearrange("p a b c -> p (a b c)"), 0.0)
    ts(OPADF, OPAD, FLAG[:, 0:1], ALU.mult)
    for l in range(L):
        c0 = l * NCOL
        act(ESK[:, l * 8:(l + 1) * 8], COLS[:, c0 + 34:c0 + 42], AF.Exp)
        ts(PSH[:, l * 8:(l + 1) * 8], COLS[:, c0 + 26:c0 + 34], 0.5, ALU.mult)
    def xload(g_, dc_, nbuf=2, base=6):
        k_ = base + (dc_ % nbuf)
        fa = F(k_)
        dma("sp", fa, xT_d[dc_][:, g_ * G:(g_ + 1) * G], ("c_x", k_))
        act(XH(dc_, g_), fa, AF.Copy)
        tt(XL(dc_, g_), fa, XH(dc_, g_), ALU.subtract)

    for dc in range(DC):
        xload(0, dc, nbuf=8, base=0)
    dma("sp", CORR.rearrange("p a b -> p (a b)"), corr_d, "c_corr")
    dma("sp", F(3), mask_d, "c_mask")
    ts(F(3), F(3), 240000.0, ALU.mult, -240000.0, ALU.add)
    for j in range(8):
        fa = F(4 + (j % 2))
        dma("sp", fa, biasT_d[j], ("c_bias", j % 2))
        stt(ET[:, j, :], fa, 8.0, F(3), ALU.mult, ALU.add)
    dma("sp", F(2)[:, 0:128], ident_d, "c_ident")
    cp(IDENT, F(2)[:, 0:128])

    deferred = []

    def drain(n):
        while deferred and n > 0:
            deferred.pop(0)()
            n -= 1

    for g_ in range(1, NG):
        for dc_ in range(DC):
            deferred.append(lambda g_=g_, dc_=dc_: xload(g_, dc_))

    def layer_group(l, g, last_layer):
        c0 = l * NCOL
        i_full = l + 5 - L
        if g == 0:
            i_f = min(max(i_full, 0), GT)
            i_k = max(i_f - 1, 0)
        else:
            i_f = i_k = 0
        carry_only = i_f >= GT
        cf, ck = 128 * i_f, 128 * i_k

        def col(k):
            return COLS[:, c0 + k:c0 + k + 1]

        def c(ap):
            return ap[:, cf:]

        Xg = [XH(dc, g) for dc in range(DC)]

        def hchunk(cid, bank, cc=None):
            cc = cf if cc is None else cc
            W = wload(("wA", l, cid))
            mm(PS(bank)[:, cc:], [(W[:, dc, :], Xg[dc][:, cc:]) for dc in range(DC)])

        hchunk(CH_KD, 0, ck)
        act(KD[:, 0, 1 + i_k:5, :].rearrange("p a b -> p (a b)"), PS(0)[:, ck:], AF.Identity, bias=col(8))
        W = wload(("wA", l, CH_V))
        for i in range(i_k, GT):
            pairs = [(Xg[dc][:, i * 128:(i + 1) * 128], W[:, dc, :]) for dc in range(DC)]
            pairs.append((R1, ROWB[:, 0:128]))
            mm(PS(2)[:, i * 128:(i + 1) * 128], pairs)
        for i in range(i_k, GT):
            cp(VP[:, 1 + i, :, 64:128], PS(2)[:, i * 128:(i + 1) * 128].rearrange("p (a b) -> p a b", a=2), eng="act")

        if not carry_only:
            for b_ in range(2):
                memset(QAB[64:128, b_, :, 0, :], 0.0, eng="pool")
                memset(QAB[0:64, b_, :, 1, :], 0.0, eng="pool")
            hb = {"n": 0}
            blocks = list(range(i_f, GT))
            nb = len(blocks)
            seq = []
            for t_ in range(nb + 2):
                if 2 <= t_ and t_ - 2 < nb:
                    seq.append(("D", blocks[t_ - 2]))
                if t_ < nb:
                    seq.append(("S", blocks[t_]))
            done_d = [x[1] for x in seq if x[0] == "D"]
            for b_ in blocks:
                if b_ not in done_d:
                    seq.append(("D", b_))
            k1 = 0
            while k1 < len(seq) and seq[k1][0] == "S":
                k1 += 1
            rest = seq[k1:]
            st1, st3, st5 = seq[:k1], rest[:len(rest) // 2], rest[len(rest) // 2:]

            def next_hbank():
                b_ = (0, 1, 4)[hb["n"] % 3]
                hb["n"] += 1
                return b_

            def attn_chunk(j, qa, qb, gbank):
                den, num = PS(5), PS(6 + (j % 2))
                sb = [2, 3]

                def s_block(i):
                    sp_ = PS(sb[i % 2])
                    skip_prev = (g == 0 and i == 0)
                    items = [(sp_, IDENT, ET[:, j, :])]
                    qsrc = qa[:, i, :, :].rearrange("p a b -> p (a b)")
                    for pc in range(2):
                        if skip_prev and pc == 0:
                            continue
                        items.append((sp_[:, pc * 256:(pc + 1) * 256], KD[:, 0, i + pc, :], qsrc))
                    rd = []
                    for o_, a_, b_ in items:
                        rd += [a_, b_]
                    n_ = len(items)

                    def fn(e, items=items, n_=n_):
                        inst = None
                        for ii, (o_, a_, b_) in enumerate(items):
                            inst = e.matmul(o_, lhsT=a_, rhs=b_, start=(ii == 0), stop=(ii == n_ - 1))
                        return inst
                    S_.add("pe", fn, reads=rd, writes=[sp_])
                    act(PT2[:, i % 2, :], sp_, AF.Exp, scale=0.125)

                def dn_block(i):
                    p2 = PT2[:, i % 2, :]
                    dpairs, npairs = [], []
                    for hd in range(2):
                        for pc in range(2):
                            if g == 0 and i == 0 and pc == 0:
                                continue
                            rhs = p2[:, (pc * 2 + hd) * 128:(pc * 2 + hd + 1) * 128]
                            op_ = OPADF if (g == 1 and i == 0 and pc == 0) else OPAD
                            lo_ = 64 if hd == 0 else 0
                            dpairs.append((op_[:, lo_:lo_ + 128], rhs))
                            npairs.append((VP[:, i + pc, hd, lo_:lo_ + 128], rhs))
                    mm(den[:, i * 128:(i + 1) * 128], dpairs)
                    mm(num[:, i * 128:(i + 1) * 128], npairs)

                def gate_pre():
                    k2 = 2 * (j % 2)
                    f0, f1 = c(F(k2)), c(F(k2 + 1))
                    act(f0, c(PS(gbank)), AF.Identity, bias=col(10 + j))
                    act(f1, f0, AF.Tanh, scale=0.5)

                def gate_chain():
                    k2 = 2 * (j % 2)
                    f0, f1 = c(F(k2)), c(F(k2 + 1))
                    stt(f1, f1, 1.0, f0, ALU.add, ALU.mult)
                    act(f0, c(den), AF.Identity, bias=ESK[:, l * 8 + j:l * 8 + j + 1])
                    recip(f0, f0)
                    tt(f0, f0, f1, ALU.mult)
                    stt(c(AB[:, j, :]), c(num), 0.5, f0, ALU.mult, ALU.mult)

                def run(stage):
                    for kind, i in stage:
                        (s_block if kind == "S" else dn_block)(i)

                return run, gate_chain, gate_pre

            pend = None
            for j in range(8):
                qbank = next_hbank()
                hchunk(CH_Q + j, qbank)
                qa = QAB[:, j % 2, :, :, :]
                qb = qa
                act(QAB[0:64, j % 2, i_f:, 0, :], PS(qbank)[0:64, cf:].rearrange("p (a b) -> p a b", b=128),
                    AF.Identity, bias=COLS[0:64, c0 + j:c0 + j + 1])
                act(QAB[64:128, j % 2, i_f:, 1, :], PS(qbank)[64:128, cf:].rearrange("p (a b) -> p a b", b=128),
                    AF.Identity, bias=COLS[64:128, c0 + j:c0 + j + 1])
                if pend is not None:
                    pend[0](st3)
                gbank = next_hbank()
                hchunk(CH_GA + j, gbank)
                if pend is not None:
                    pend[0](st5)
                nxt = attn_chunk(j, qa, qb, gbank)
                nxt[2]()
                nxt[0](st1)
                if pend is not None:
                    pend[1]()
                pend = nxt
            pend[0](st3)
            pend[0](st5)
            pend[1]()

        ubs = {"n": 0}

        def pool_u(pg):
            w = POOL_WINDOWS[pg]
            nsteps = {2: 1, 4: 2, 8: 3, 16: 4}[w]
            for cc in range(2):
                cu = 2 * pg + cc
                bank = cu % 2
                hchunk(CH_U + cu, bank, ck)
                ua = UB[:, ubs["n"] % 2, :]
                ubs["n"] += 1
                tb = [UB[:, 2, :], UB[:, 3, :]]
                act(ua[:, 16 + ck:], PS(bank)[:, ck:], AF.Identity, bias=col(74 + cu))
                if g >= 1:
                    cp(ua[:, 0:16], UCY[:, cu, :], eng="dve")
                else:
                    memset(ua[:, 0:16], 0.0)
                if g < NG - 1:
                    if g == 0:
                        ts(UCY[:, cu, :], ua[:, 512:528], FLAG[:, 0:1], ALU.mult)
                    else:
                        cp(UCY[:, cu, :], ua[:, 512:528], eng="dve")
                if carry_only:
                    continue
                lo = 0 if g >= 1 else 16 + ck
                src = ua
                sh = 1
                for st_ in range(nsteps):
                    dst = tb[st_ % 2]
                    lo2 = lo + sh
                    tt(dst[:, lo2:], src[:, lo2:], src[:, lo2 - sh:528 - sh], ALU.add)
                    src, lo, sh = dst, lo2, sh * 2
                if g == 1:
                    tt(src[:, 16:32], src[:, 16:32], CORR[:, pg, :], ALU.mult)
                stt(c(DF[:, pg % 2, cc, :]), src[:, 16 + cf:], 1.0 / w, ua[:, 16 + cf:], ALU.mult, ALU.subtract)

        def pool_mix(pg):
            s_ = small["wp"] % 2
            small["wp"] += 1
            dma("pool", WPL[:, s_, :], wP_d[l, pg], ("wp", s_))
            Wp = WPL[:, s_, :].rearrange("p (a b) -> p a b", a=2)
            gbanks = []
            for dd in range(2):
                jb = 2 * pg + dd
                hchunk(CH_GP + jb, 6 + dd)
            for dd in range(2):
                mbank = 4 + dd
                mm(c(PS(mbank)), [(Wp[:, kc, dd * 128:(dd + 1) * 128], c(DF[:, pg % 2, kc, :])) for kc in range(2)])
                jb = 2 * pg + dd
                gbank = 6 + dd
                k2 = 2 * (jb % 2)
                f0, f1 = c(F(k2)), c(F(k2 + 1))
                act(f0, c(PS(gbank)), AF.Identity, bias=col(18 + jb))
                act(f1, f0, AF.Tanh, scale=0.5)
                stt(f1, f1, 1.0, f0, ALU.add, ALU.mult)
                stt(c(AB[:, 8 + jb, :]), c(PS(mbank)), PSH[:, l * 8 + jb:l * 8 + jb + 1], f1, ALU.mult, ALU.mult)
                drain(1)

        for pg in range(5):
            if pg < 4:
                pool_u(pg)
            if pg > 0 and not carry_only:
                pool_mix(pg - 1)

        if g < NG - 1:
            cp(KD[:, 0, 0, :], KD[:, 0, 4, :], eng="dve")
            if g == 0:
                ts(VP[:, 0, :, :], VP[:, 4, :, :], FLAG[:, 0:1], ALU.mult)
            else:
                cp(VP[:, 0, :, :], VP[:, 4, :, :], eng="dve")
        if carry_only:
            return

        pb = 0
        dma("pool", PTB[:, pb, :, :], pT_d[l][:, :, g * G:(g + 1) * G], ("pt", pb))
        for dch in range(DC):
            W = wload(("wG", l, dch))
            gb = dch % 2
            mm(c(PS(gb)), [(W[:, dc, :], c(Xg[dc])) for dc in range(DC)])
            s_ = small["we"] % 2
            small["we"] += 1
            dma("pool", WPE[:, s_, :], wE_d[l, dch], ("we", s_))
            We = WPE[:, s_, :].rearrange("p (a b) -> p a b", a=2)
            eb = 2 + dch % 2
            mm(c(PS(eb)), [(We[:, pc, :], c(PTB[:, pb, pc, :])) for pc in range(2)])
            f0 = c(F(dch % 2))
            act(f0, c(PS(gb)), AF.Tanh, scale=0.5)
            stt(c(PLE[:, dch, :]), f0, 1.0, c(PS(eb)), ALU.add, ALU.mult)
            drain(2 if (l == 0 and g == 0) else 1)

        stats_pend = []

        def stats_mm(dch, yh, ys):
            S_.add("pe", (lambda e: e.matmul(c(PS(6)), lhsT=ONES, rhs=yh, start=(dch == 0), stop=(dch == DC - 1))),
                   reads=[ONES, yh], writes=[c(PS(6))])
            S_.add("pe", (lambda e: e.matmul(c(PS(7)), lhsT=ONES, rhs=ys, start=(dch == 0), stop=(dch == DC - 1))),
                   reads=[ONES, ys], writes=[c(PS(7))])

        for dch in range(DC):
            W = wload(("wO", l, dch))
            mb = 4 + dch % 2
            mm(c(PS(mb)), [(W[:, cc, :], c(AB[:, cc, :])) for cc in range(DC)])
            if len(stats_pend) >= 1:
                stats_mm(*stats_pend.pop(0))
            f0 = c(F(2 + dch % 2))
            tt(f0, c(XH(dch, g)), c(XL(dch, g)), ALU.add)
            stt(f0, f0, ALPHA, c(PS(mb)), ALU.mult, ALU.add)
            y = c(YV(dch, g))
            stt(y, c(PLE[:, dch, :]), 0.5, f0, ALU.mult, ALU.add)
            yh, ys = c(YH[:, dch % 2, :]), c(YS[:, dch % 2, :])
            act(yh, y, AF.Copy)
            act(ys, y, AF.Square)
            stats_pend.append((dch, yh, ys))
        while stats_pend:
            stats_mm(*stats_pend.pop(0))

        drain(10 ** 6)
        mean, var, r, tmp = c(F(7)), c(F(4)), c(F(6)), c(F(5))
        ts(mean, c(PS(6)), 1.0 / D, ALU.mult)
        tt(tmp, mean, mean, ALU.mult)
        stt(var, c(PS(7)), 1.0 / D, tmp, ALU.mult, ALU.subtract)
        ts(var, var, LN_EPS, ALU.add)
        ts(r, var, 0.25, ALU.mult, 1.0, ALU.add)
        recip(r, r)
        for _ in range(5):
            tt(tmp, r, r, ALU.mult)
            tt(tmp, tmp, var, ALU.mult)
            ts(tmp, tmp, -0.5, ALU.mult, 1.5, ALU.add)
            tt(r, r, tmp, ALU.mult)
        tt(mean, mean, r, ALU.mult)
        def ln_a(dch):
            y = c(YV(dch, g))
            tt(y, y, r, ALU.mult)
            tt(y, y, mean, ALU.subtract)

        def ln_b(dch):
            y = c(YV(dch, g))
            f1 = c(F(4 + dch % 2))
            act(f1, y, AF.Identity, bias=col(58 + dch), scale=col(42 + dch))
            if last_layer:
                dma("sp", out_d[dch][:, (g - 1) * G + cf:g * G], f1, ("out", dch % 2))
            else:
                act(c(XH(dch, g)), f1, AF.Copy)
                tt(c(XL(dch, g)), f1, c(XH(dch, g)), ALU.subtract)

        for k in range(DC + 2):
            def item(k=k):
                if k < DC:
                    ln_a(k)
                if k >= 2:
                    ln_b(k - 2)
            deferred.append(item)

    for l in range(L):
        dma("pool", ROWB[0:1, :], rows_d[l:l + 1, :], "rows")
        for g in range(NG):
            layer_group(l, g, l == L - 1)

    drain(10 ** 6)
    if record:
        stack.close()
        return wrec
    S_.emit(nc, stack, final_dma_keys=[("out", 0), ("out", 1)])
    print("arena top", A.top, "of", ARENA_BYTES, "n_ops", {e: len(v) for e, v in S_.ops.items()})
    stack.close()
    return nc


def _t5_bucket(dist):
    max_exact = 16
    d = np.maximum(dist, 0)
    d_f = np.maximum(d, 1).astype(np.float64)
    large = max_exact + (np.log(d_f / max_exact) / math.log(128 / max_exact) * (32 - max_exact)).astype(np.int64)
    large = np.minimum(large, 31)
    return np.where(d < max_exact, d, large)


def _const_tables():
    k = np.arange(128)[:, None]
    q = np.arange(128)[None, :]
    dist = np.zeros((128, 4, 128), np.int64)
    mask = np.zeros((128, 4, 128), np.float32)
    for hd in range(2):
        for pc in range(2):
            d = q + 128 - (k + 128 * pc)
            sl = pc * 2 + hd
            dist[:, sl, :] = d
            mask[:, sl, :] = ((d >= 0) & (d < 128)).astype(np.float32)
    bucket = _t5_bucket(np.clip(dist, 0, 255))
    j = np.arange(128)[:, None]
    t = np.arange(128)[None, :]
    ptm = np.zeros((128, 8, 128), np.float32)
    ptf = np.zeros((128, 4, 128), np.float32)
    for pg, w in enumerate(POOL_WINDOWS):
        ptm[:, pg, :] = np.where(j - 128 >= t - w + 1, 1.0 / w, 0.0)
        cur = np.where((j <= t) & (j >= t - w + 1), 1.0 / w, 0.0) - (j == t)
        ptm[:, 4 + pg, :] = cur
        cnt = np.minimum(t + 1, w).astype(np.float64)
        ptf[:, pg, :] = (np.where((j <= t) & (j >= t - w + 1), 1.0 / cnt, 0.0) - (j == t)).astype(np.float32)
    corr = np.ones((4, 16), np.float64)
    for pg, w in enumerate(POOL_WINDOWS):
        tt_ = np.arange(16)
        corr[pg] = w / np.minimum(tt_ + 1, w)
    corr_first = np.broadcast_to(corr.reshape(1, 64), (128, 64)).astype(np.float32).copy()
    return bucket, mask.reshape(128, 512), corr_first


def _layer_weights(inp, l):
    w_in = np.asarray(inp["w_in"][l], np.float32)
    b_in = np.asarray(inp["b_in"][l], np.float32)

    def chunk(cols):
        w = w_in[:, cols]
        return w.reshape(DC, 128, 128).transpose(1, 0, 2).reshape(128, 2048)

    wA = np.empty((NA, 128, 2048), np.float32)
    wA[CH_KD] = chunk(np.arange(1024, 1152))
    wA[CH_KD + 1] = wA[CH_KD]

    def hpair(j):
        return np.concatenate([np.arange(64 * j, 64 * j + 64), np.arange(512 + 64 * j, 512 + 64 * j + 64)])
    wA[CH_V] = chunk(np.arange(1152, 1280))
    for j in range(8):
        wA[CH_Q + j] = chunk(hpair(j))
        wA[CH_GA + j] = chunk(1280 + hpair(j))
        wA[CH_U + j] = chunk(np.arange(2304 + 128 * j, 2304 + 128 * j + 128))
        wA[CH_GP + j] = chunk(np.arange(3328 + 128 * j, 3328 + 128 * j + 128))
    wp = np.asarray(inp["w_pool"][l], np.float32)
    wP = wp.reshape(4, 2, 128, 256).transpose(0, 2, 1, 3).reshape(4, 128, 512)

    def ochunks(w, kc):
        r = w.reshape(kc, 128, DC, 128).transpose(2, 1, 0, 3)
        return np.ascontiguousarray(r).reshape(DC, 128, kc * 128)

    wG = ochunks(np.asarray(inp["w_gate_ple"][l], np.float32), 16)
    rperm = np.concatenate([hpair(j) for j in range(8)] + [np.arange(1024, 2048)])
    wO = ochunks(np.asarray(inp["w_out"][l], np.float32)[rperm], 16)
    wE = ochunks(np.asarray(inp["w_ple"][l], np.float32), 2)
    cols = np.empty((128, NCOL), np.float32)
    p = np.arange(128)
    for j in range(8):
        cols[:, j] = b_in[hpair(j)]
        cols[:, 10 + j] = b_in[1280 + hpair(j)]
        cols[:, 18 + j] = b_in[3328 + 128 * j + p]
        cols[:, 26 + j] = np.asarray(inp["pool_scale"][l], np.float32)[128 * j + p]
        sk = np.asarray(inp["attn_sinks"][l], np.float32)
        cols[:, 34 + j] = np.where(p < 64, sk[j], sk[8 + j])
    cols[:, 8] = b_in[1024 + p]
    cols[:, 9] = b_in[1024 + p]
    for dch in range(DC):
        cols[:, 42 + dch] = np.asarray(inp["ln_gain"][l], np.float32)[128 * dch + p]
        cols[:, 58 + dch] = np.asarray(inp["ln_bias"][l], np.float32)[128 * dch + p]
    for j in range(8):
        cols[:, 74 + j] = b_in[2304 + 128 * j + p]
    rows = np.concatenate([b_in[1152:1280], b_in[2304:3328]]).astype(np.float32)
    return wA, wP, wG, wO, wE, cols, rows


_PROG_CACHE = {}


def _get_prog(n_layers):
    if n_layers not in _PROG_CACHE:
        if USE_WLOOK:
            wl = build_program(n_layers, record=True)
            _PROG_CACHE[n_layers] = build_program(n_layers, wlist=wl)
        else:
            _PROG_CACHE[n_layers] = build_program(n_layers)
    return _PROG_CACHE[n_layers]


def _run_layers(x_full, inp, layers):
    nl = len(layers)
    nc = _get_prog(nl)
    bucket, mask01, corr_first = _const_tables()
    rel_bias = np.asarray(inp["rel_bias"], np.float32)
    gathered = rel_bias[bucket]
    biasT = np.empty((8, 128, 4, 128), np.float32)
    for j in range(8):
        for hd in range(2):
            for pc in range(2):
                sl = pc * 2 + hd
                biasT[j, :, sl, :] = gathered[:, sl, :, j + 8 * hd]
    biasT = biasT.reshape(8, 128, 512)
    corr_std = np.ones((128, 64), np.float32)

    lw = [_layer_weights(inp, l) for l in layers]
    wA = np.stack([w[0] for w in lw])
    wP = np.stack([w[1] for w in lw])
    wG = np.stack([w[2] for w in lw])
    wO = np.stack([w[3] for w in lw])
    wE = np.stack([w[4] for w in lw])
    cols = np.concatenate([w[5] for w in lw], axis=1)
    rows = np.stack([w[6] for w in lw])
    p_all = np.asarray(inp["p"], np.float32)

    in_maps = []
    for c in range(NCORES):
        b, ch = c // 4, c % 4
        t0 = ch * CHUNK
        xs = np.zeros((T, D), np.float32)
        ps = np.zeros((nl, T, 256), np.float32)
        lo = t0 - HALO
        src0 = max(lo, 0)
        xs[src0 - lo:, :] = x_full[b, src0:t0 + CHUNK, :]
        for li, l in enumerate(layers):
            ps[li, src0 - lo:, :] = p_all[l, b, src0:t0 + CHUNK, :]
        xT = np.ascontiguousarray(xs.T).reshape(DC, 128, T)
        pT = np.ascontiguousarray(ps.transpose(0, 2, 1)).reshape(nl, 2, 128, T).transpose(0, 2, 1, 3)
        first = ch == 0
        in_maps.append({
            "xT": xT, "pT": np.ascontiguousarray(pT),
            "wA": wA, "wP": wP, "wG": wG, "wO": wO, "wE": wE,
            "cols": np.ascontiguousarray(cols), "rows": rows,
            "biasT": biasT, "mask01": mask01,
            "corr": corr_first if first else corr_std,
            "flag": np.full((128, 8), 0.0 if first else 1.0, np.float32),
            "ident": np.eye(128, dtype=np.float32),
        })
    res = run_bass_kernel_spmd(nc, in_maps, core_ids=list(range(NCORES)))
    out = np.empty((B, S, D), np.float32)
    for c in range(NCORES):
        b, ch = c // 4, c % 4
        oT = np.asarray(res.results[c]["outT"]).reshape(D, CHUNK)
        out[b, ch * CHUNK:(ch + 1) * CHUNK, :] = oT.T
    return out


FUSED = True


def kernel(**inputs):
    x = np.asarray(inputs["x"], np.float32)
    if FUSED:
        return _run_layers(x, inputs, list(range(DEPTH)))
    for l in range(DEPTH):
        x = _run_layers(x, inputs, [l])
    return x
```
